# Optimizing a Trainium2 kernel written in Bass

```python
import jax
import jax.numpy as jnp
from jax import lax
import numpy as np

D_MODEL = 1024
BATCH = 4
SEQ = 8192
DEPTH = 4

N_META = 16
LRU_WIDTH = 512
LRU_HEADS = 8
LRU_HEAD_DIM = LRU_WIDTH // LRU_HEADS
CONV_WIDTH = 4
LRU_C = 8.0
MLA_HEADS = 8
Q_LORA = 384
KV_LORA = 256
QK_NOPE = 64
QK_ROPE = 32
QK_DIM = QK_NOPE + QK_ROPE
V_DIM = 64
MLA_WIDTH = MLA_HEADS * V_DIM
ROPE_THETA = 10000.0
ATTN_BLOCK = 128
D_MIX = LRU_WIDTH + MLA_WIDTH
SPLIT_1 = LRU_WIDTH
SPLIT_2 = 2 * LRU_WIDTH
SPLIT_3 = 2 * LRU_WIDTH + Q_LORA
SPLIT_4 = 2 * LRU_WIDTH + Q_LORA + KV_LORA
D_IN = 2 * LRU_WIDTH + Q_LORA + KV_LORA + QK_ROPE
PEER_HEADS = 8
N_KEYS = 128
N_EXPERTS = N_KEYS * N_KEYS
PEER_TOPK = 16
D_QUERY = 256
D_SUBKEY = D_QUERY // 2
PEER_CHUNK = 128
EPS = 1e-6
NEG_INF = -1e30

kernel_name = "hymba_rglru_mla_peer_trunk"


def rms_norm(x, g):
    xf = x.astype(jnp.float32)
    y = xf * lax.rsqrt(jnp.mean(xf * xf, axis=-1, keepdims=True) + EPS)
    return (y * g.astype(jnp.float32)).astype(x.dtype)


def rope(x, pos):
    half = x.shape[-1] // 2
    freq = ROPE_THETA ** (-jnp.arange(half, dtype=jnp.float32) / half)
    ang = pos[:, None] * freq[None, :]
    cos = jnp.cos(ang)[:, None, :]
    sin = jnp.sin(ang)[:, None, :]
    xf = x.astype(jnp.float32)
    x1, x2 = xf[..., :half], xf[..., half:]
    return jnp.concatenate([x1 * cos - x2 * sin, x2 * cos + x1 * sin], axis=-1).astype(x.dtype)


def causal_conv(x, w, b):
    T = x.shape[1]
    xp = jnp.pad(x, ((0, 0), (CONV_WIDTH - 1, 0), (0, 0)))
    out = xp[:, 0:T] * w[0]
    for k in range(1, CONV_WIDTH):
        out = out + xp[:, k:k + T] * w[k]
    return out + b


def rg_lru(xc, wa, ba, wx, bx, lam):
    B, T, W = xc.shape
    xb = xc.reshape(B, T, LRU_HEADS, LRU_HEAD_DIM)
    r = jax.nn.sigmoid(jnp.einsum('bthi,hij->bthj', xb, wa) + ba).reshape(B, T, W)
    i = jax.nn.sigmoid(jnp.einsum('bthi,hij->bthj', xb, wx) + bx).reshape(B, T, W)
    log_a = -LRU_C * r.astype(jnp.float32) * jax.nn.softplus(-lam.astype(jnp.float32))
    a = jnp.exp(log_a)
    mult = jnp.sqrt(-jnp.expm1(2.0 * log_a))
    bterm = mult * (i * xc).astype(jnp.float32)

    def combine(left, right):
        a1, b1 = left
        a2, b2 = right
        return a1 * a2, a2 * b1 + b2

    _, h = lax.associative_scan(combine, (a, bterm), axis=1)
    return h.astype(xc.dtype)


def causal_block_attention(q, k, v):
    B, T, H, _ = q.shape
    n_pad = (-T) % ATTN_BLOCK
    padw = ((0, 0), (n_pad, 0), (0, 0), (0, 0))
    q, k, v = jnp.pad(q, padw), jnp.pad(k, padw), jnp.pad(v, padw)
    Tp = T + n_pad
    nb = Tp // ATTN_BLOCK
    qb = jnp.moveaxis(q.reshape(B, nb, ATTN_BLOCK, H, QK_DIM), 1, 0)
    kpos = jnp.arange(Tp)
    scale = 1.0 / np.sqrt(QK_DIM).astype(np.float32)

    def one_block(args):
        q_blk, b_idx = args
        qpos = b_idx * ATTN_BLOCK + jnp.arange(ATTN_BLOCK)
        s = jnp.einsum('bqhd,bkhd->bhqk', q_blk, k).astype(jnp.float32) * scale
        mask = (kpos[None, :] <= qpos[:, None]) & (kpos[None, :] >= n_pad)
        s = jnp.where(mask[None, None], s, NEG_INF)
        p = jax.nn.softmax(s, axis=-1).astype(v.dtype)
        return jnp.einsum('bhqk,bkhd->bqhd', p, v)

    o = lax.map(one_block, (qb, jnp.arange(nb)))
    o = jnp.moveaxis(o, 0, 1).reshape(B, Tp, H, V_DIM)
    return o[:, n_pad:]


def mla_group(c_q, c_kv, k_r, q_ln_g, w_uq, kv_ln_g, w_ukv, q_hn_g, k_hn_g):
    B, T, _ = c_q.shape
    pos = jnp.arange(T, dtype=jnp.float32)
    q = (rms_norm(c_q, q_ln_g) @ w_uq).reshape(B, T, MLA_HEADS, QK_DIM)
    kv = (rms_norm(c_kv, kv_ln_g) @ w_ukv).reshape(B, T, MLA_HEADS, QK_NOPE + V_DIM)
    k_nope, v = kv[..., :QK_NOPE], kv[..., QK_NOPE:]
    k_rope = jnp.broadcast_to(k_r[:, :, None, :], (B, T, MLA_HEADS, QK_ROPE))
    k = jnp.concatenate([k_nope, k_rope], axis=-1)
    q = rms_norm(q, q_hn_g)
    k = rms_norm(k, k_hn_g)
    q = jnp.concatenate([q[..., :QK_NOPE], rope(q[..., QK_NOPE:], pos)], axis=-1)
    k = jnp.concatenate([k[..., :QK_NOPE], rope(k[..., QK_NOPE:], pos)], axis=-1)
    o = causal_block_attention(q, k, v)
    return o.reshape(B, T, MLA_WIDTH)


def peer(h, wq, subkeys, u, v):
    B, T, D = h.shape
    q = (h @ wq).reshape(B, T, PEER_HEADS, 2, D_SUBKEY)
    s = jnp.einsum('bthpd,pkd->bthpk', q, subkeys).astype(jnp.float32)
    top_s, top_i = lax.top_k(s, PEER_TOPK)
    cand_s = (top_s[..., 0, :, None] + top_s[..., 1, None, :]).reshape(B, T, PEER_HEADS, PEER_TOPK * PEER_TOPK)
    cand_i = (top_i[..., 0, :, None] * N_KEYS + top_i[..., 1, None, :]).reshape(B, T, PEER_HEADS, PEER_TOPK * PEER_TOPK)
    best_s, best_pos = lax.top_k(cand_s, PEER_TOPK)
    experts = jnp.take_along_axis(cand_i, best_pos, axis=-1)
    gates = jax.nn.softmax(best_s, axis=-1).astype(h.dtype)
    n_tok = B * T
    n_chunks = -(-n_tok // PEER_CHUNK)
    pad = n_chunks * PEER_CHUNK - n_tok
    hk = PEER_HEADS * PEER_TOPK
    hf = jnp.pad(h.reshape(n_tok, D), ((0, pad), (0, 0))).reshape(n_chunks, PEER_CHUNK, D)
    ef = jnp.pad(experts.reshape(n_tok, hk), ((0, pad), (0, 0))).reshape(n_chunks, PEER_CHUNK, hk)
    gf = jnp.pad(gates.reshape(n_tok, hk), ((0, pad), (0, 0))).reshape(n_chunks, PEER_CHUNK, hk)

    def one_chunk(args):
        hc, ec, gc = args
        uc = jnp.take(u, ec, axis=0)
        act = jax.nn.gelu(jnp.einsum('cd,ckd->ck', hc, uc), approximate=False)
        vc = jnp.take(v, ec, axis=0)
        return jnp.einsum('ck,ckd->cd', gc * act, vc)

    out = lax.map(one_chunk, (hf, ef, gf)).reshape(n_chunks * PEER_CHUNK, D)
    return out[:n_tok].reshape(B, T, D)


def setup_inputs(seed: int = 0) -> dict:
    key = jax.random.key(seed)
    ks = jax.random.split(key, 32)
    f32 = jnp.float32

    def nrm(k, shape, scale):
        return jax.random.normal(k, shape, f32) * scale

    def gain(k, n):
        return 1.0 + 0.01 * jax.random.normal(k, (DEPTH, n), f32)

    u0 = jax.random.uniform(ks[9], (DEPTH, LRU_WIDTH), f32, 0.9, 0.999)
    a0 = u0 ** (1.0 / LRU_C)
    lru_lambda = jnp.log(a0) - jnp.log1p(-a0)
    return {
        "x": nrm(ks[0], (BATCH, SEQ, D_MODEL), 1.0),
        "meta_tokens": nrm(ks[1], (N_META, D_MODEL), 1.0),
        "mix_norm_g": gain(ks[2], D_MODEL),
        "w_in": nrm(ks[3], (DEPTH, D_MODEL, D_IN), D_MODEL ** -0.5),
        "conv_w": nrm(ks[4], (DEPTH, CONV_WIDTH, LRU_WIDTH), CONV_WIDTH ** -0.5),
        "conv_b": nrm(ks[5], (DEPTH, LRU_WIDTH), 0.01),
        "lru_wa": nrm(ks[6], (DEPTH, LRU_HEADS, LRU_HEAD_DIM, LRU_HEAD_DIM), LRU_HEAD_DIM ** -0.5),
        "lru_ba": nrm(ks[7], (DEPTH, LRU_HEADS, LRU_HEAD_DIM), 0.01),
        "lru_wx": nrm(ks[8], (DEPTH, LRU_HEADS, LRU_HEAD_DIM, LRU_HEAD_DIM), LRU_HEAD_DIM ** -0.5),
        "lru_bx": nrm(ks[10], (DEPTH, LRU_HEADS, LRU_HEAD_DIM), 0.01),
        "lru_lambda": lru_lambda,
        "q_lora_norm_g": gain(ks[11], Q_LORA),
        "w_uq": nrm(ks[12], (DEPTH, Q_LORA, MLA_HEADS * QK_DIM), Q_LORA ** -0.5),
        "kv_lora_norm_g": gain(ks[13], KV_LORA),
        "w_ukv": nrm(ks[14], (DEPTH, KV_LORA, MLA_HEADS * (QK_NOPE + V_DIM)), KV_LORA ** -0.5),
        "q_head_norm_g": gain(ks[15], QK_DIM),
        "k_head_norm_g": gain(ks[16], QK_DIM),
        "lru_out_norm_g": gain(ks[17], LRU_WIDTH),
        "attn_out_norm_g": gain(ks[18], MLA_WIDTH),
        "w_out": nrm(ks[19], (DEPTH, D_MIX, D_MODEL), D_MIX ** -0.5),
        "ffn_norm_g": gain(ks[20], D_MODEL),
        "peer_wq": nrm(ks[21], (DEPTH, D_MODEL, PEER_HEADS * D_QUERY), D_MODEL ** -0.5),
        "peer_subkeys": nrm(ks[22], (DEPTH, 2, N_KEYS, D_SUBKEY), D_SUBKEY ** -0.5),
        "peer_u": nrm(ks[23], (DEPTH, N_EXPERTS, D_MODEL), D_MODEL ** -0.5),
        "peer_v": nrm(ks[24], (DEPTH, N_EXPERTS, D_MODEL), D_MODEL ** -0.5),
    }


def reference(x, meta_tokens, mix_norm_g, w_in, conv_w, conv_b, lru_wa, lru_ba, lru_wx, lru_bx,
              lru_lambda, q_lora_norm_g, w_uq, kv_lora_norm_g, w_ukv, q_head_norm_g, k_head_norm_g,
              lru_out_norm_g, attn_out_norm_g, w_out, ffn_norm_g, peer_wq, peer_subkeys, peer_u, peer_v):
    B = x.shape[0]
    meta = jnp.broadcast_to(meta_tokens.astype(x.dtype)[None], (B, N_META, D_MODEL))
    h = jnp.concatenate([meta, x], axis=1)
    for l in range(DEPTH):
        z = rms_norm(h, mix_norm_g[l]) @ w_in[l]
        x_lru = z[..., :SPLIT_1]
        gate = z[..., SPLIT_1:SPLIT_2]
        c_q = z[..., SPLIT_2:SPLIT_3]
        c_kv = z[..., SPLIT_3:SPLIT_4]
        k_r = z[..., SPLIT_4:]
        xc = causal_conv(x_lru, conv_w[l], conv_b[l])
        y_lru = rg_lru(xc, lru_wa[l], lru_ba[l], lru_wx[l], lru_bx[l], lru_lambda[l]) * jax.nn.gelu(gate, approximate=False)
        y_att = mla_group(c_q, c_kv, k_r, q_lora_norm_g[l], w_uq[l], kv_lora_norm_g[l], w_ukv[l],
                          q_head_norm_g[l], k_head_norm_g[l])
        mixed = jnp.concatenate([rms_norm(y_lru, lru_out_norm_g[l]), rms_norm(y_att, attn_out_norm_g[l])], axis=-1)
        h = h + mixed @ w_out[l]
        h = h + peer(rms_norm(h, ffn_norm_g[l]), peer_wq[l], peer_subkeys[l], peer_u[l], peer_v[l])
    return h[:, N_META:]
```

```python
import numpy as np
import concourse.bass as bass
import concourse.mybir as mybir
from concourse.bass_utils import run_bass_kernel_spmd
from contextlib import ExitStack

F32 = mybir.dt.float32
BF16 = mybir.dt.bfloat16
AF = mybir.ActivationFunctionType
ALU = mybir.AluOpType
AX = mybir.AxisListType

D = 1024
DIN = 1696
EPS = 1e-6
COMPUTE = ("pe", "act", "dve", "pool")
NDMASEM = {"sp": 24, "pool": 8, "act": 8}


class Op:
    __slots__ = ("eng", "fn", "deps", "is_dma", "sig", "sigval", "dsem", "prevwait")

    def __init__(self, eng, fn, is_dma):
        self.eng = eng
        self.fn = fn
        self.is_dma = is_dma
        self.deps = []
        self.sig = False
        self.sigval = 0
        self.dsem = None
        self.prevwait = None


class Sync:
    def __init__(self, nc, es):
        self.nc = nc
        self.csem = {e: es.enter_context(nc.semaphore("c_" + e)) for e in COMPUTE}
        self.dsem = {}
        for q, n in NDMASEM.items():
            for i in range(n):
                self.dsem[(q, i)] = es.enter_context(nc.semaphore("d_%s%d" % (q, i)))
        self.cnt = {e: 0 for e in COMPUTE}
        self.dcnt = {k: 0 for k in self.dsem}
        self.dma_rr = {q: 0 for q in NDMASEM}
        self.waited = {e: {} for e in ("pe", "act", "dve", "pool", "sp")}


class Prog:
    def __init__(self, sy):
        self.sy = sy
        self.nc = sy.nc
        self.ops = []
        self.streams = {"pe": [], "act": [], "dve": [], "pool": [], "sp": []}
        self.lastw = {}
        self.readers = {}

    def op(self, eng, fn, reads=(), writes=(), dma=False):
        o = Op(eng, fn, dma)
        deps = set()
        for r in reads:
            w = self.lastw.get(r)
            if w is not None:
                deps.add(w)
        for r in writes:
            w = self.lastw.get(r)
            if w is not None:
                deps.add(w)
            for rd in self.readers.get(r, ()):
                deps.add(rd)
        fl = []
        for d in deps:
            if (not d.is_dma) and (not dma) and d.eng == eng and eng == "pe":
                continue
            fl.append(d)
        o.deps = fl
        for r in reads:
            self.readers.setdefault(r, []).append(o)
        for r in writes:
            self.lastw[r] = o
            self.readers[r] = []
        if dma:
            sy = self.sy
            k = sy.dma_rr[eng]
            sy.dma_rr[eng] = k + 1
            slot = (eng, k % NDMASEM[eng])
            prev = sy.dcnt[slot]
            sy.dcnt[slot] = prev + 16
            o.dsem = (slot, prev + 16)
            o.prevwait = (slot, prev) if prev else None
        self.ops.append(o)
        self.streams[eng].append(o)
        return o

    def pe(self, fn, reads=(), writes=()):
        return self.op("pe", fn, reads, writes)

    def act(self, fn, reads=(), writes=()):
        return self.op("act", fn, reads, writes)

    def dve(self, fn, reads=(), writes=()):
        return self.op("dve", fn, reads, writes)

    def pool(self, fn, reads=(), writes=()):
        return self.op("pool", fn, reads, writes)

    def dma(self, fn, reads=(), writes=(), q="sp"):
        return self.op(q, fn, reads, writes, dma=True)

    def emit(self):
        nc = self.nc
        sy = self.sy
        for o in self.ops:
            for d in o.deps:
                if not d.is_dma:
                    d.sig = True
        for e in COMPUTE:
            for o in reversed(self.streams[e]):
                if not o.is_dma:
                    o.sig = True
                    break
        for o in self.ops:
            if not o.is_dma and o.sig:
                sy.cnt[o.eng] += 1
                o.sigval = sy.cnt[o.eng]
        fin_c = dict(sy.cnt)
        fin_d = dict(sy.dcnt)
        with nc.Block() as block:
            engmap = {"pe": block.tensor, "act": block.scalar, "dve": block.vector,
                      "pool": block.gpsimd, "sp": block.sync}

            def run_stream(ename, e):
                waited = sy.waited[ename]
                for o in self.streams[ename]:
                    need = {}
                    for d in o.deps:
                        if d.is_dma:
                            key = ("d", d.dsem[0])
                            v = d.dsem[1]
                        else:
                            key = ("c", d.eng)
                            v = d.sigval
                        if waited.get(key, 0) >= v:
                            continue
                        if need.get(key, 0) < v:
                            need[key] = v
                    if o.is_dma and o.prevwait is not None:
                        key = ("d", o.prevwait[0])
                        v = o.prevwait[1]
                        if waited.get(key, 0) < v and need.get(key, 0) < v:
                            need[key] = v
                    for key, v in need.items():
                        s = sy.dsem[key[1]] if key[0] == "d" else sy.csem[key[1]]
                        e.wait_ge(s, v)
                        waited[key] = v
                    ins = o.fn(e)
                    if o.is_dma:
                        ins.then_inc(sy.dsem[o.dsem[0]], 16)
                    elif o.sig:
                        ins.then_inc(sy.csem[o.eng], 1)
                for f in COMPUTE:
                    if f != ename and fin_c[f] > waited.get(("c", f), 0):
                        e.wait_ge(sy.csem[f], fin_c[f])
                        waited[("c", f)] = fin_c[f]
                for slot, v in fin_d.items():
                    if v > waited.get(("d", slot), 0):
                        e.wait_ge(sy.dsem[slot], v)
                        waited[("d", slot)] = v

            for ename in ("sp", "pe", "act", "dve", "pool"):
                def mk(ename):
                    def body(e):
                        run_stream(ename, e)
                    return body
                engmap[ename](mk(ename))


class Rot:
    def __init__(self, tiles, name):
        self.tiles = tiles
        self.name = name
        self.i = 0

    def next(self):
        k = self.i % len(self.tiles)
        self.i += 1
        return self.tiles[k], "%s%d" % (self.name, k)


def build(NB, DEPTH, NE=128):
    NP = NB * 128
    SEQ = NP - 128
    nc = bass.Bass("TRN2", target_bir_lowering=False)
    inp = {}
    sfx = [""]

    def SBT(name, shape, dt):
        return nc.sbuf_tensor(name + sfx[0], shape, dt)

    def PST(name, shape, dt):
        return nc.psum_tensor(name + sfx[0], shape, dt)

    def din(name, shape, dt=F32):
        inp[name] = nc.dram_tensor(name, list(shape), dt, kind="ExternalInput").ap()
        return inp[name]

    x = din("x", [SEQ, D])
    meta = din("meta_tokens", [16, D])
    mix_g = din("mix_norm_g", [DEPTH, D])
    w_in = din("w_in", [DEPTH, D, DIN])
    conv_w = din("conv_w", [DEPTH, 4, 512])
    conv_b = din("conv_b", [DEPTH, 512])
    lru_wa = din("lru_wa", [DEPTH, 8, 64, 64])
    lru_ba = din("lru_ba", [DEPTH, 512])
    lru_wx = din("lru_wx", [DEPTH, 8, 64, 64])
    lru_bx = din("lru_bx", [DEPTH, 512])
    lru_lam = din("lru_lambda", [DEPTH, 512])
    qlg = din("q_lora_norm_g", [DEPTH, 384])
    w_uq = din("w_uq", [DEPTH, 384, 768])
    kvlg = din("kv_lora_norm_g", [DEPTH, 256])
    w_ukv = din("w_ukv", [DEPTH, 256, 1024])
    qhg = din("q_head_norm_g", [DEPTH, 96])
    khg = din("k_head_norm_g", [DEPTH, 96])
    log_ = din("lru_out_norm_g", [DEPTH, 512])
    aog = din("attn_out_norm_g", [DEPTH, 512])
    w_out = din("w_out", [DEPTH, D, D])
    ffn_g = din("ffn_norm_g", [DEPTH, D])
    peer_wq = din("peer_wq", [DEPTH, D, 2048])
    peer_sk = din("peer_subkeys", [DEPTH, 2, 128, 128])
    peer_u = din("peer_u", [DEPTH, 16384, D])
    peer_v = din("peer_v", [DEPTH, 16384, D])
    c_ident = din("c_ident", [128, 128])
    c_mask = din("c_mask", [4, 128, 512])
    c_ropeC = din("c_ropeC", [96, NP])
    c_ropeS = din("c_ropeS", [96, NP])
    c_rot = din("c_rot", [96, 96])
    c_padb = din("c_padb", [128, 1])
    out = nc.dram_tensor("out", [SEQ, D], F32, kind="ExternalOutput").ap()

    def scr(name, shape, dt):
        return nc.dram_tensor(name, list(shape), dt, kind="Internal").ap()

    hT = scr("hT", [D, NP], F32)
    ylT = scr("ylT", [512, NP], F32)
    yaT = scr("yaT", [512, NP], F32)
    qTd = scr("qTd", [8, 96, NP], BF16)
    kTd = scr("kTd", [8, 96, NP], BF16)
    vvd = scr("vvd", [NP, 8, 64], BF16)
    hnTd = scr("hnTd", [D, NP], BF16)
    pqTd = scr("pqTd", [16, 128, NP], BF16)
    UTd = scr("UTd", [128, 128, 8, 128], BF16)
    VBd = scr("VBd", [128, 128, D], BF16)

    tiles = []
    p = 0
    while p < NP:
        w = min(512, NP - p)
        tiles.append((p, w))
        p += w

    es0 = ExitStack()
    with es0:
        sy = Sync(nc, es0)
        identf = es0.enter_context(SBT("identf", [128, 128], F32))
        identb = es0.enter_context(SBT("identb", [128, 128], BF16))
        onesb = es0.enter_context(SBT("onesb", [128, 128], BF16))
        onesf = es0.enter_context(SBT("onesf", [128, 128], F32))
        epsb = es0.enter_context(SBT("epsb", [128, 1], F32))
        padb = es0.enter_context(SBT("padb", [128, 1], F32))

        def hT3(ap):
            return ap.rearrange("(c p) n -> p c n", p=128)

        with ExitStack() as es:
            P = Prog(sy)
            sb = lambda n, s, d=F32: es.enter_context(SBT(n, s, d))
            xin = [sb("xin%d" % i, [128, D]) for i in range(2)]
            xo = [sb("xo%d" % i, [128, 8, 128]) for i in range(2)]
            pt = [es.enter_context(PST("p0t%d" % i, [128, 4, 128], F32)) for i in range(4)]
            P.dma(lambda e: e.dma_start(out=identf[:], in_=c_ident[:, :]), writes=["identf"])
            P.dma(lambda e: e.dma_start(out=padb[:], in_=c_padb[:, :]), writes=["padb"])
            P.dve(lambda e: e.tensor_copy(out=identb[:], in_=identf[:]), reads=["identf"], writes=["identb"])
            P.pool(lambda e: e.memset(onesb[:], 1.0), writes=["onesb"])
            P.pool(lambda e: e.memset(onesf[:], 1.0), writes=["onesf"])
            P.pool(lambda e: e.memset(epsb[:], EPS), writes=["epsb"])
            for b in range(NB):
                xi, xk = xin[b % 2], "xin%d" % (b % 2)
                xoo, xok = xo[b % 2], "xo%d" % (b % 2)
                if b == 0:
                    P.pool(lambda e, xi=xi: e.memset(xi[:], 0.0), writes=[xk])
                    P.dma(lambda e, xi=xi: e.dma_start(out=xi[112:128, :], in_=meta[:, :]), writes=[xk])
                else:
                    P.dma(lambda e, xi=xi, b=b: e.dma_start(out=xi[:], in_=x[(b - 1) * 128:b * 128, :]), writes=[xk])
                for hf in range(2):
                    ptt, ptk = pt[(2 * b + hf) % 4], "p0t%d" % ((2 * b + hf) % 4)
                    for c in range(4):
                        cc = hf * 4 + c
                        P.pe(lambda e, ptt=ptt, c=c, cc=cc, xi=xi: e.transpose(out=ptt[:, c, :], in_=xi[:, cc * 128:(cc + 1) * 128], identity=identf[:]),
                             reads=[xk, "identf"], writes=[ptk])
                    if hf == 0:
                        P.dve(lambda e, ptt=ptt, xoo=xoo: e.tensor_copy(out=xoo[:, 0:4, :], in_=ptt[:]), reads=[ptk], writes=[xok])
                    else:
                        P.act(lambda e, ptt=ptt, xoo=xoo: e.activation(out=xoo[:, 4:8, :], in_=ptt[:], func=AF.Copy), reads=[ptk], writes=[xok])
                P.dma(lambda e, xoo=xoo, b=b: e.dma_start(out=hT3(hT)[:, :, b * 128:(b + 1) * 128], in_=xoo[:]), reads=[xok], q="pool")
            P.emit()

        def colvec_load(P, es, name, src2d, rows, n, ptile, ptk):
            pp = min(n, 128)
            ncol = n // pp
            st = es.enter_context(SBT(name + "_st", [rows, n], F32))
            res = es.enter_context(SBT(name, [pp, ncol, rows], F32))
            P.dma(lambda e: e.dma_start(out=st[:], in_=src2d), writes=[name + "_st"])
            for c in range(ncol):
                P.pe(lambda e, c=c: e.transpose(out=ptile[0:pp, c, 0:rows], in_=st[0:rows, c * pp:(c + 1) * pp], identity=identf[0:rows, 0:rows]),
                     reads=[name + "_st", "identf"], writes=[ptk])
            P.dve(lambda e: e.tensor_copy(out=res[:], in_=ptile[0:pp, 0:ncol, 0:rows]), reads=[ptk], writes=[name])
            return res

        def rms_rstd(P, ps_t, psk, sq_chunks, sqk, nch, w, inv_n, rt, rtk, rr, rrk, parts=128):
            for c in range(nch):
                P.pe(lambda e, c=c: e.matmul(ps_t[0:parts, 0:w], lhsT=onesb[:, 0:parts], rhs=sq_chunks(c), start=(c == 0), stop=(c == nch - 1)),
                     reads=[sqk, "onesb"], writes=[psk])
            P.act(lambda e: e.activation(out=rt[0:parts, 0:w], in_=ps_t[0:parts, 0:w], func=AF.Sqrt, scale=inv_n, bias=epsb[0:parts, :]),
                  reads=[psk, "epsb"], writes=[rtk])
            P.dve(lambda e: e.reciprocal(out=rr[0:parts, 0:w], in_=rt[0:parts, 0:w]), reads=[rtk], writes=[rrk])

        for l in range(DEPTH):
            sfx[0] = "_L%d" % l
            with ExitStack() as es:
                P = Prog(sy)
                sb = lambda n, s, d=F32: es.enter_context(SBT(n, s, d))
                psum = [es.enter_context(PST("pA%d" % i, [128, 512], F32)) for i in range(7)]
                psr = Rot(psum[0:7], "pA")
                pvec = es.enter_context(PST("pAv", [128, 8, 8], F32))
                gm = colvec_load(P, es, "gm", mix_g[l:l + 1, :], 1, D, pvec, "pAv")
                v512 = sb("v512", [128, 4, 8])
                st512 = sb("st512", [8, 512])
                P.dma(lambda e: e.dma_start(out=st512[0:4, :], in_=conv_w[l, :, :]), writes=["st512"])
                for r_, src in ((4, conv_b), (5, lru_ba), (6, lru_bx), (7, lru_lam)):
                    P.dma(lambda e, r_=r_, src=src: e.dma_start(out=st512[r_:r_ + 1, :], in_=src[l:l + 1, :]), writes=["st512"])
                for g in range(4):
                    P.pe(lambda e, g=g: e.transpose(out=pvec[:, g, 0:8], in_=st512[0:8, g * 128:(g + 1) * 128], identity=identf[0:8, 0:8]),
                         reads=["st512", "identf", "gm"], writes=["pAv"])
                P.dve(lambda e: e.tensor_copy(out=v512[:], in_=pvec[:, 0:4, 0:8]), reads=["pAv"], writes=["v512"])
                spt = sb("spt", [128, 4, 1])
                nc8 = sb("nc8", [128, 4, 1])
                nc16 = sb("nc16", [128, 4, 1])
                P.act(lambda e: e.activation(out=spt[:], in_=v512[:, :, 7:8], func=AF.Exp, scale=-1.0), reads=["v512"], writes=["spt"])
                P.act(lambda e: e.activation(out=spt[:], in_=spt[:], func=AF.Ln, bias=onesf[:, 0:1]), reads=["spt", "onesf"], writes=["spt"])
                P.dve(lambda e: e.tensor_scalar(out=nc8[:], in0=spt[:], scalar1=-8.0, scalar2=None, op0=ALU.mult), reads=["spt"], writes=["nc8"])
                P.dve(lambda e: e.tensor_scalar(out=nc16[:], in0=spt[:], scalar1=-16.0, scalar2=None, op0=ALU.mult), reads=["spt"], writes=["nc16"])
                gq = colvec_load(P, es, "gq", qlg[l:l + 1, :], 1, 384, pvec, "pAv")
                gkv = colvec_load(P, es, "gkv", kvlg[l:l + 1, :], 1, 256, pvec, "pAv")
                gqh = colvec_load(P, es, "gqh", qhg[l:l + 1, :], 1, 96, pvec, "pAv")
                gkh = colvec_load(P, es, "gkh", khg[l:l + 1, :], 1, 96, pvec, "pAv")
                wst = [sb("wst%d" % i, [128, DIN]) for i in range(2)]
                winb = sb("winb", [128, 8, 1760], BF16)
                P.pool(lambda e: e.memset(winb[:, :, 1664:1728], 0.0), writes=["winb"])
                for c in range(8):
                    ws, wk = wst[c % 2], "wst%d" % (c % 2)
                    P.dma(lambda e, ws=ws, c=c: e.dma_start(out=ws[:], in_=w_in[l, c * 128:(c + 1) * 128, :]), writes=[wk])
                    P.dve(lambda e, ws=ws, c=c: e.tensor_scalar(out=winb[:, c, 0:1664], in0=ws[:, 0:1664], scalar1=gm[:, c, :], scalar2=None, op0=ALU.mult),
                          reads=[wk, "gm"], writes=["winb"])
                    P.dve(lambda e, ws=ws, c=c: e.tensor_scalar(out=winb[:, c, 1728:1760], in0=ws[:, 1664:1696], scalar1=gm[:, c, :], scalar2=None, op0=ALU.mult),
                          reads=[wk, "gm"], writes=["winb"])
                wuqb = sb("wuqb", [128, 3, 768], BF16)
                for c in range(3):
                    ws, wk = wst[c % 2], "wst%d" % (c % 2)
                    P.dma(lambda e, ws=ws, c=c: e.dma_start(out=ws[:, 0:768], in_=w_uq[l, c * 128:(c + 1) * 128, :]), writes=[wk])
                    P.dve(lambda e, ws=ws, c=c: e.tensor_scalar(out=wuqb[:, c, :], in0=ws[:, 0:768], scalar1=gq[:, c, :], scalar2=None, op0=ALU.mult),
                          reads=[wk, "gq"], writes=["wuqb"])
                wukb = sb("wukb", [128, 2, 8, 64], BF16)
                wuvb = sb("wuvb", [128, 2, 8, 64], BF16)
                for c in range(2):
                    ws, wk = wst[(c + 1) % 2], "wst%d" % ((c + 1) % 2)
                    P.dma(lambda e, ws=ws, c=c: e.dma_start(out=ws[:, 0:1024], in_=w_ukv[l, c * 128:(c + 1) * 128, :]), writes=[wk])
                    wv_ = ws[:, 0:1024].rearrange("p (h t d) -> p h t d", h=8, t=2)
                    P.dve(lambda e, wv_=wv_, c=c: e.tensor_scalar(out=wukb[:, c, :, :], in0=wv_[:, :, 0, :], scalar1=gkv[:, c, :], scalar2=None, op0=ALU.mult),
                          reads=[wk, "gkv"], writes=["wukb"])
                    P.dve(lambda e, wv_=wv_, c=c: e.tensor_scalar(out=wuvb[:, c, :, :], in0=wv_[:, :, 1, :], scalar1=gkv[:, c, :], scalar2=None, op0=ALU.mult),
                          reads=[wk, "gkv"], writes=["wuvb"])
                wab = sb("wab", [128, 4, 128], BF16)
                wxb = sb("wxb", [128, 4, 128], BF16)
                wgs = sb("wgs", [128, 2, 4, 128])
                P.pool(lambda e: e.memset(wgs[:], 0.0), writes=["wgs"])
                for t_, src in ((0, lru_wa), (1, lru_wx)):
                    for g in range(4):
                        for hh in range(2):
                            P.dma(lambda e, t_=t_, src=src, g=g, hh=hh: e.dma_start(out=wgs[hh * 64:(hh + 1) * 64, t_, g, hh * 64:(hh + 1) * 64], in_=src[l, 2 * g + hh, :, :]),
                                  writes=["wgs"])
                P.dve(lambda e: e.tensor_copy(out=wab[:], in_=wgs[:, 0, :, :]), reads=["wgs"], writes=["wab"])
                P.dve(lambda e: e.tensor_copy(out=wxb[:], in_=wgs[:, 1, :, :]), reads=["wgs"], writes=["wxb"])
                rotf = sb("rotf", [96, 96])
                P.dma(lambda e: e.dma_start(out=rotf[:], in_=c_rot[:, :]), writes=["rotf"])
                hin = [sb("hin%d" % i, [128, 8, 512]) for i in range(1)]
                hsq = sb("hsq", [128, 8, 512], BF16)
                hnb = sb("hnb", [128, 8, 512], BF16)
                rt = sb("rtA", [128, 512])
                rz = sb("rzA", [128, 512])
                xlb = [sb("xlb%d" % g, [128, 3 + 512]) for g in range(4)]
                zg = sb("zg", [128, 4, 512])
                zq = sb("zq", [128, 3, 512])
                zkv = sb("zkv", [128, 2, 512])
                zkr = sb("zkr", [96, 512])
                state = sb("state", [128, 4, 1])
                P.pool(lambda e: e.memset(state[:], 0.0), writes=["state"])
                for g in range(4):
                    P.pool(lambda e, g=g: e.memset(xlb[g][:, 0:3], 0.0), writes=["xlb%d" % g])
                tA = [sb("tA%d" % i, [128, 512]) for i in range(7)]
                tAb = [sb("tAb%d" % i, [128, 512], BF16) for i in range(1)]
                cqs = sb("cqs", [128, 3, 512], BF16)
                cqb = sb("cqb", [128, 3, 512], BF16)
                cks = sb("cks", [128, 2, 512], BF16)
                ckb = sb("ckb", [128, 2, 512], BF16)
                rq = sb("rq", [128, 512])
                rkv = sb("rkv", [128, 512])
                rkc = sb("rkc", [128, 4, 1])
                ropeC = sb("ropeC", [96, 512])
                ropeS = sb("ropeS", [96, 512])
                raw = [sb("raw%d" % i, [96, 512]) for i in range(2)]
                rsq = sb("rsq", [96, 512], BF16)
                hs_ = sb("hs_", [96, 512])
                qn = sb("qn", [96, 512])
                qa = sb("qa", [96, 512])
                qb_ = sb("qb_", [96, 512])
                qf = [sb("qf%d" % i, [96, 512], BF16) for i in range(2)]
                vst = [sb("vst%d" % i, [128, 512], BF16) for i in range(2)]
                ylo = [sb("ylo%d" % i, [128, 512]) for i in range(2)]
                qfr = Rot(qf, "qf")
                rawr = Rot(raw, "raw")
                vstr = Rot(vst, "vst")
                ylor = Rot(ylo, "ylo")

                def head_norm_rope(rw, rwk, gvec, gk, dst, w, p0):
                    P.act(lambda e: e.activation(out=rsq[:, 0:w], in_=rw[:, 0:w], func=AF.Square), reads=[rwk], writes=["rsq"])
                    pm, pmk = psr.next()
                    P.pe(lambda e: e.matmul(pm[0:96, 0:w], lhsT=onesb[0:96, 0:96], rhs=rsq[:, 0:w], start=True, stop=True), reads=["rsq", "onesb"], writes=[pmk])
                    P.act(lambda e: e.activation(out=hs_[:, 0:w], in_=pm[0:96, 0:w], func=AF.Sqrt, scale=1.0 / 96, bias=epsb[0:96, :]), reads=[pmk, "epsb"], writes=["hs_"])
                    P.dve(lambda e: e.reciprocal(out=hs_[:, 0:w], in_=hs_[:, 0:w]), reads=["hs_"], writes=["hs_"])
                    P.dve(lambda e: e.scalar_tensor_tensor(out=qn[:, 0:w], in0=rw[:, 0:w], scalar=gvec[:, 0, :], in1=hs_[:, 0:w], op0=ALU.mult, op1=ALU.mult),
                          reads=[rwk, "hs_", gk], writes=["qn"])
                    pr, prk = psr.next()
                    P.pe(lambda e: e.matmul(pr[0:96, 0:w], lhsT=rotf[:, :], rhs=qn[:, 0:w], start=True, stop=True), reads=["qn", "rotf"], writes=[prk])
                    P.pool(lambda e: e.tensor_tensor(out=qa[:, 0:w], in0=qn[:, 0:w], in1=ropeC[:, 0:w], op=ALU.mult), reads=["qn", "ropeC"], writes=["qa"])
                    P.dve(lambda e: e.tensor_tensor(out=qb_[:, 0:w], in0=pr[0:96, 0:w], in1=ropeS[:, 0:w], op=ALU.mult), reads=[prk, "ropeS"], writes=["qb_"])
                    qft, qfk = qfr.next()
                    P.pool(lambda e: e.tensor_tensor(out=qft[:, 0:w], in0=qa[:, 0:w], in1=qb_[:, 0:w], op=ALU.add), reads=["qa", "qb_"], writes=[qfk])
                    P.dma(lambda e: e.dma_start(out=dst[:, p0:p0 + w], in_=qft[:, 0:w]), reads=[qfk])

                for ti, (p0, w) in enumerate(tiles):
                    hi, hik = hin[0], "hin0"
                    P.dma(lambda e, hi=hi, p0=p0, w=w: e.dma_start(out=hi[:, :, 0:w], in_=hT3(hT)[:, :, p0:p0 + w]), writes=[hik])
                    P.dma(lambda e, p0=p0, w=w: e.dma_start(out=ropeC[:, 0:w], in_=c_ropeC[:, p0:p0 + w]), writes=["ropeC"], q="pool")
                    P.dma(lambda e, p0=p0, w=w: e.dma_start(out=ropeS[:, 0:w], in_=c_ropeS[:, p0:p0 + w]), writes=["ropeS"], q="pool")
                    P.act(lambda e, hi=hi, w=w: e.activation(out=hsq[:, :, 0:w], in_=hi[:, :, 0:w], func=AF.Square), reads=[hik], writes=["hsq"])
                    ps_, psk = psr.next()
                    rms_rstd(P, ps_, psk, lambda c, w=w: hsq[:, c, 0:w], "hsq", 8, w, 1.0 / D, rt, "rtA", rz, "rzA")
                    P.dve(lambda e, hi=hi, w=w: e.tensor_tensor(out=hnb[:, :, 0:w], in0=hi[:, :, 0:w], in1=rz[:, 0:w].unsqueeze(1).to_broadcast([128, 8, w]), op=ALU.mult),
                          reads=[hik, "rzA"], writes=["hnb"])
                    for oc in range(14):
                        pz, pzk = psr.next()
                        M = 96 if oc == 13 else 128
                        c0 = oc * 128
                        for c in range(8):
                            P.pe(lambda e, pz=pz, c=c, c0=c0, M=M, w=w: e.matmul(pz[0:M, 0:w], lhsT=winb[:, c, c0:c0 + M], rhs=hnb[:, c, 0:w], start=(c == 0), stop=(c == 7)),
                                 reads=["winb", "hnb"], writes=[pzk])
                        if oc < 4:
                            dst, dk = xlb[oc][:, 3:3 + w], "xlb%d" % oc
                        elif oc < 8:
                            dst, dk = zg[:, oc - 4, 0:w], "zg"
                        elif oc < 11:
                            dst, dk = zq[:, oc - 8, 0:w], "zq"
                        elif oc < 13:
                            dst, dk = zkv[:, oc - 11, 0:w], "zkv"
                        else:
                            dst, dk = zkr[:, 0:w], "zkr"
                        if oc % 2 == 0:
                            P.act(lambda e, pz=pz, dst=dst, M=M, w=w: e.activation(out=dst, in_=pz[0:M, 0:w], func=AF.Copy), reads=[pzk], writes=[dk])
                        else:
                            P.dve(lambda e, pz=pz, dst=dst, M=M, w=w: e.tensor_copy(out=dst, in_=pz[0:M, 0:w]), reads=[pzk], writes=[dk])
                        if oc < 4 and ti == 0:
                            P.pool(lambda e, oc=oc: e.memset(xlb[oc][:, 3:3 + 112], 0.0), writes=["xlb%d" % oc])
                    for g in range(4):
                        xk = "xlb%d" % g
                        xb_ = xlb[g]
                        xc, a_, a2, it, bt, hs2, gg = tA[0], tA[1], tA[2], tA[3], tA[4], tA[5], tA[6]
                        P.dve(lambda e, xb_=xb_, g=g, w=w: e.tensor_scalar(out=xc[:, 0:w], in0=xb_[:, 0:w], scalar1=v512[:, g, 0:1], scalar2=v512[:, g, 4:5], op0=ALU.mult, op1=ALU.add),
                              reads=[xk, "v512"], writes=["xc"])
                        for k in range(1, 4):
                            P.dve(lambda e, xb_=xb_, g=g, k=k, w=w: e.scalar_tensor_tensor(out=xc[:, 0:w], in0=xb_[:, k:k + w], scalar=v512[:, g, k:k + 1], in1=xc[:, 0:w], op0=ALU.mult, op1=ALU.add),
                                  reads=[xk, "v512", "xc"], writes=["xc"])
                        P.pool(lambda e, xb_=xb_, w=w: e.tensor_copy(out=xb_[:, 0:3], in_=xb_[:, w:w + 3]), reads=[xk], writes=[xk])
                        P.pool(lambda e, w=w: e.tensor_copy(out=tAb[0][:, 0:w], in_=xc[:, 0:w]), reads=["xc"], writes=["xcb"])
                        pr_, prk = psr.next()
                        pi_, pik = psr.next()
                        P.pe(lambda e, pr_=pr_, g=g, w=w: e.matmul(pr_[:, 0:w], lhsT=wab[:, g, :], rhs=tAb[0][:, 0:w], start=True, stop=True), reads=["wab", "xcb"], writes=[prk])
                        P.pe(lambda e, pi_=pi_, g=g, w=w: e.matmul(pi_[:, 0:w], lhsT=wxb[:, g, :], rhs=tAb[0][:, 0:w], start=True, stop=True), reads=["wxb", "xcb"], writes=[pik])
                        P.act(lambda e, pr_=pr_, g=g, w=w: e.activation(out=a_[:, 0:w], in_=pr_[:, 0:w], func=AF.Sigmoid, bias=v512[:, g, 5:6]), reads=[prk, "v512"], writes=["a_"])
                        P.act(lambda e, pi_=pi_, g=g, w=w: e.activation(out=it[:, 0:w], in_=pi_[:, 0:w], func=AF.Sigmoid, bias=v512[:, g, 6:7]), reads=[pik, "v512"], writes=["it"])
                        P.act(lambda e, g=g, w=w: e.activation(out=a2[:, 0:w], in_=a_[:, 0:w], func=AF.Exp, scale=nc16[:, g, :]), reads=["a_", "nc16"], writes=["a2"])
                        P.act(lambda e, g=g, w=w: e.activation(out=a_[:, 0:w], in_=a_[:, 0:w], func=AF.Exp, scale=nc8[:, g, :]), reads=["a_", "nc8", "a2"], writes=["a_"])
                        P.dve(lambda e, w=w: e.tensor_scalar(out=a2[:, 0:w], in0=a2[:, 0:w], scalar1=-1.0, scalar2=1.0, op0=ALU.mult, op1=ALU.add), reads=["a2"], writes=["a2"])
                        P.act(lambda e, w=w: e.activation(out=a2[:, 0:w], in_=a2[:, 0:w], func=AF.Sqrt), reads=["a2"], writes=["a2"])
                        P.dve(lambda e, w=w: e.tensor_tensor(out=bt[:, 0:w], in0=it[:, 0:w], in1=xc[:, 0:w], op=ALU.mult), reads=["it", "xc"], writes=["bt"])
                        P.dve(lambda e, w=w: e.tensor_tensor(out=bt[:, 0:w], in0=bt[:, 0:w], in1=a2[:, 0:w], op=ALU.mult), reads=["bt", "a2"], writes=["bt"])
                        if ti == 0:
                            P.dve(lambda e: e.memset(bt[:, 0:112], 0.0), reads=["bt"], writes=["bt"])
                        P.dve(lambda e, g=g, w=w: e.tensor_tensor_scan(out=hs2[:, 0:w], data0=a_[:, 0:w], data1=bt[:, 0:w], initial=state[:, g, :], op0=ALU.mult, op1=ALU.add),
                              reads=["a_", "bt", "state"], writes=["hs2"])
                        P.dve(lambda e, g=g, w=w: e.tensor_copy(out=state[:, g, :], in_=hs2[:, w - 1:w]), reads=["hs2"], writes=["state"])
                        P.act(lambda e, g=g, w=w: e.activation(out=gg[:, 0:w], in_=zg[:, g, 0:w], func=AF.Gelu), reads=["zg"], writes=["gg"])
                        yo, yok = ylor.next()
                        P.pool(lambda e, yo=yo, w=w: e.tensor_tensor(out=yo[:, 0:w], in0=hs2[:, 0:w], in1=gg[:, 0:w], op=ALU.mult), reads=["hs2", "gg"], writes=[yok])
                        P.dma(lambda e, yo=yo, g=g, p0=p0, w=w: e.dma_start(out=ylT[g * 128:(g + 1) * 128, p0:p0 + w], in_=yo[:, 0:w]), reads=[yok])
                    P.act(lambda e, w=w: e.activation(out=cqs[:, :, 0:w], in_=zq[:, :, 0:w], func=AF.Square), reads=["zq"], writes=["cqs"])
                    ps_, psk = psr.next()
                    rms_rstd(P, ps_, psk, lambda c, w=w: cqs[:, c, 0:w], "cqs", 3, w, 1.0 / 384, rt, "rtA", rq, "rq")
                    P.dve(lambda e, w=w: e.tensor_tensor(out=cqb[:, :, 0:w], in0=zq[:, :, 0:w], in1=rq[:, 0:w].unsqueeze(1).to_broadcast([128, 3, w]), op=ALU.mult),
                          reads=["zq", "rq"], writes=["cqb"])
                    P.act(lambda e, w=w: e.activation(out=cks[:, :, 0:w], in_=zkv[:, :, 0:w], func=AF.Square), reads=["zkv"], writes=["cks"])
                    ps_, psk = psr.next()
                    rms_rstd(P, ps_, psk, lambda c, w=w: cks[:, c, 0:w], "cks", 2, w, 1.0 / 256, rt, "rtA", rkv, "rkv")
                    P.dve(lambda e, w=w: e.tensor_tensor(out=ckb[:, :, 0:w], in0=zkv[:, :, 0:w], in1=rkv[:, 0:w].unsqueeze(1).to_broadcast([128, 2, w]), op=ALU.mult),
                          reads=["zkv", "rkv"], writes=["ckb"])
                    for h in range(8):
                        pu, puk = psr.next()
                        for c in range(3):
                            P.pe(lambda e, pu=pu, c=c, h=h, w=w: e.matmul(pu[0:96, 0:w], lhsT=wuqb[:, c, h * 96:(h + 1) * 96], rhs=cqb[:, c, 0:w], start=(c == 0), stop=(c == 2)),
                                 reads=["wuqb", "cqb"], writes=[puk])
                        rw, rwk = rawr.next()
                        P.act(lambda e, pu=pu, rw=rw, w=w: e.activation(out=rw[:, 0:w], in_=pu[0:96, 0:w], func=AF.Copy), reads=[puk], writes=[rwk])
                        head_norm_rope(rw, rwk, gqh, "gqh", qTd[h], w, p0)
                    for h in range(8):
                        pu, puk = psr.next()
                        for c in range(2):
                            P.pe(lambda e, pu=pu, c=c, h=h, w=w: e.matmul(pu[0:64, 0:w], lhsT=wukb[:, c, h, :], rhs=ckb[:, c, 0:w], start=(c == 0), stop=(c == 1)),
                                 reads=["wukb", "ckb"], writes=[puk])
                        rw, rwk = rawr.next()
                        P.act(lambda e, pu=pu, rw=rw, w=w: e.activation(out=rw[0:64, 0:w], in_=pu[0:64, 0:w], func=AF.Copy), reads=[puk], writes=[rwk])
                        P.pool(lambda e, rw=rw, w=w: e.tensor_copy(out=rw[64:96, 0:w], in_=zkr[64:96, 0:w]), reads=["zkr"], writes=[rwk])
                        head_norm_rope(rw, rwk, gkh, "gkh", kTd[h], w, p0)
                    for b in range(w // 128):
                        pv, pvk = psr.next()
                        for c in range(2):
                            P.pe(lambda e, pv=pv, c=c, b=b: e.matmul(pv[:, 0:512], lhsT=ckb[:, c, b * 128:(b + 1) * 128], rhs=wuvb[:, c, :, :].rearrange("p h d -> p (h d)"), start=(c == 0), stop=(c == 1)),
                                 reads=["wuvb", "ckb"], writes=[pvk])
                        vs, vsk = vstr.next()
                        P.act(lambda e, pv=pv, vs=vs: e.activation(out=vs[:], in_=pv[:, 0:512], func=AF.Copy), reads=[pvk], writes=[vsk])
                        P.dma(lambda e, vs=vs, b=b, p0=p0: e.dma_start(out=vvd[p0 + b * 128:p0 + (b + 1) * 128, :, :].rearrange("t h d -> t (h d)"), in_=vs[:]), reads=[vsk])
                P.emit()

            with ExitStack() as es:
                P = Prog(sy)
                sb = lambda n, s, d=F32: es.enter_context(SBT(n, s, d))
                ps_s = [es.enter_context(PST("pBs%d" % i, [128, 512], F32)) for i in range(3)]
                ps_o = [es.enter_context(PST("pBo%d" % i, [128, 512], F32)) for i in range(2)]
                ps_b = [es.enter_context(PST("pBb%d" % i, [128, 512], F32)) for i in range(2)]
                psr_s, psr_o, psr_b = Rot(ps_s, "pBs"), Rot(ps_o, "pBo"), Rot(ps_b, "pBb")
                maskf = sb("maskf", [128, 4, 512])
                maskb = sb("maskb", [128, 4, 512], BF16)
                P.dma(lambda e: e.dma_start(out=maskf[:], in_=c_mask.rearrange("j k q -> k j q")), writes=["maskf"])
                P.dve(lambda e: e.tensor_copy(out=maskb[:], in_=maskf[:]), reads=["maskf"], writes=["maskb"])
                kTh = [sb("kTh%d" % i, [96, NP], BF16) for i in range(2)]
                qTh = [sb("qTh%d" % i, [96, NP], BF16) for i in range(2)]
                v1 = [sb("v1_%d" % i, [128, NB, 65], BF16) for i in range(2)]
                pT = [sb("pT%d" % i, [128, 512], BF16) for i in range(4)]
                pTr = Rot(pT, "pT")
                num = [sb("num%d" % i, [64, 512]) for i in range(2)]
                rr_ = sb("rrB", [128, 512])
                ob = [sb("ob%d" % i, [64, 512]) for i in range(2)]
                numr, obr = Rot(num, "num"), Rot(ob, "ob")
                scale = float(1.0 / np.sqrt(96.0))
                for i in range(2):
                    P.pool(lambda e, i=i: e.memset(v1[i][:, :, 64:65], 1.0), writes=["v1_%d" % i])
                for h in range(8):
                    kt, ktk = kTh[h % 2], "kTh%d" % (h % 2)
                    qt, qtk = qTh[h % 2], "qTh%d" % (h % 2)
                    vt, vtk = v1[h % 2], "v1_%d" % (h % 2)
                    P.dma(lambda e, kt=kt, h=h: e.dma_start(out=kt[:], in_=kTd[h]), writes=[ktk])
                    P.dma(lambda e, qt=qt, h=h: e.dma_start(out=qt[:], in_=qTd[h]), writes=[qtk])
                    for b0 in range(0, NB, 13):
                        b1 = min(NB, b0 + 13)
                        P.dma(lambda e, vt=vt, h=h, b0=b0, b1=b1: e.dma_start(out=vt[:, b0:b1, 0:64], in_=vvd[b0 * 128:b1 * 128, h, :].rearrange("(b p) d -> p b d", p=128)), writes=[vtk])
                    for (p0, w) in tiles:
                        nkb = (p0 + w) // 128
                        po, pok = psr_o.next()

                        def emit_s(kb, p0=p0, w=w, kt=kt, qt=qt, ktk=ktk, qtk=qtk):
                            pss, pssk = psr_s.next()
                            diag = kb * 128 >= p0
                            P.pe(lambda e: e.matmul(pss[:, 0:w], lhsT=kt[:, kb * 128:(kb + 1) * 128], rhs=qt[:, p0:p0 + w], start=True, stop=(not diag)),
                                 reads=[ktk, qtk], writes=[pssk])
                            if diag:
                                j = kb - p0 // 128
                                P.pe(lambda e: e.matmul(pss[:, 0:w], lhsT=identb[:, :], rhs=maskb[:, j, 0:w], start=False, stop=True),
                                     reads=["identb", "maskb"], writes=[pssk])
                            return pss, pssk
                        cur = emit_s(0)
                        for kb in range(nkb):
                            nxt = emit_s(kb + 1) if kb + 1 < nkb else None
                            pss, pssk = cur
                            pt_, ptk = pTr.next()
                            if kb == 0:
                                P.act(lambda e, pss=pss, pt_=pt_, w=w: e.activation(out=pt_[:, 0:w], in_=pss[:, 0:w], func=AF.Exp, scale=scale, bias=padb[:, :]),
                                      reads=[pssk, "padb"], writes=[ptk])
                            else:
                                P.act(lambda e, pss=pss, pt_=pt_, w=w: e.activation(out=pt_[:, 0:w], in_=pss[:, 0:w], func=AF.Exp, scale=scale),
                                      reads=[pssk], writes=[ptk])
                            P.pe(lambda e, po=po, vt=vt, kb=kb, pt_=pt_, w=w, nkb=nkb: e.matmul(po[0:65, 0:w], lhsT=vt[:, kb, :], rhs=pt_[:, 0:w], start=(kb == 0), stop=(kb == nkb - 1)),
                                 reads=[vtk, ptk], writes=[pok])
                            cur = nxt
                        nm, nmk = numr.next()
                        P.act(lambda e, po=po, nm=nm, w=w: e.activation(out=nm[:, 0:w], in_=po[0:64, 0:w], func=AF.Copy), reads=[pok], writes=[nmk])
                        P.dve(lambda e, po=po, w=w: e.tensor_scalar(out=rr_[64:65, 0:w], in0=po[64:65, 0:w], scalar1=1e-30, scalar2=None, op0=ALU.max), reads=[pok], writes=["rrB"])
                        P.dve(lambda e, w=w: e.reciprocal(out=rr_[64:65, 0:w], in_=rr_[64:65, 0:w]), reads=["rrB"], writes=["rrB"])
                        pb, pbk = psr_b.next()
                        P.pe(lambda e, pb=pb, w=w: e.matmul(pb[0:64, 0:w], lhsT=onesf[64:65, 0:64], rhs=rr_[64:65, 0:w], start=True, stop=True), reads=["onesf", "rrB"], writes=[pbk])
                        o_, ok_ = obr.next()
                        P.dve(lambda e, pb=pb, nm=nm, o_=o_, w=w: e.tensor_tensor(out=o_[:, 0:w], in0=pb[0:64, 0:w], in1=nm[:, 0:w], op=ALU.mult), reads=[pbk, nmk], writes=[ok_])
                        P.dma(lambda e, o_=o_, h=h, p0=p0, w=w: e.dma_start(out=yaT[h * 64:(h + 1) * 64, p0:p0 + w], in_=o_[:, 0:w]), reads=[ok_], q="pool")
                P.emit()

            with ExitStack() as es:
                P = Prog(sy)
                sb = lambda n, s, d=F32: es.enter_context(SBT(n, s, d))
                psum = [es.enter_context(PST("pC%d" % i, [128, 512], F32)) for i in range(7)]
                psr = Rot(psum[0:7], "pC")
                pvec = es.enter_context(PST("pCv", [128, 8, 8], F32))
                glo = colvec_load(P, es, "glo", log_[l:l + 1, :], 1, 512, pvec, "pCv")
                gao = colvec_load(P, es, "gao", aog[l:l + 1, :], 1, 512, pvec, "pCv")
                gff = colvec_load(P, es, "gff", ffn_g[l:l + 1, :], 1, D, pvec, "pCv")
                wst = [sb("wstC%d" % i, [128, 2048]) for i in range(2)]
                woutb = sb("woutb", [128, 8, D], BF16)
                wqb = sb("wqb", [128, 8, 2048], BF16)
                for c in range(8):
                    ws, wk = wst[c % 2], "wstC%d" % (c % 2)
                    gsrc = glo[:, c, :] if c < 4 else gao[:, c - 4, :]
                    P.dma(lambda e, ws=ws, c=c: e.dma_start(out=ws[:, 0:D], in_=w_out[l, c * 128:(c + 1) * 128, :]), writes=[wk])
                    P.dve(lambda e, ws=ws, c=c, gsrc=gsrc: e.tensor_scalar(out=woutb[:, c, :], in0=ws[:, 0:D], scalar1=gsrc, scalar2=None, op0=ALU.mult),
                          reads=[wk, "glo", "gao"], writes=["woutb"])
                for c in range(8):
                    ws, wk = wst[c % 2], "wstC%d" % (c % 2)
                    P.dma(lambda e, ws=ws, c=c: e.dma_start(out=ws[:], in_=peer_wq[l, c * 128:(c + 1) * 128, :]), writes=[wk])
                    P.pool(lambda e, ws=ws, c=c: e.tensor_scalar(out=wqb[:, c, :], in0=ws[:], scalar1=gff[:, c, :], scalar2=None, op0=ALU.mult),
                           reads=[wk, "gff"], writes=["wqb"])
                ust = [sb("ust%d" % i, [128, D]) for i in range(2)]
                vst_ = [sb("vstC%d" % i, [128, D]) for i in range(2)]
                utb = [sb("utb%d" % i, [128, 8, 128], BF16) for i in range(2)]
                vbb = [sb("vbb%d" % i, [128, D], BF16) for i in range(2)]
                u3 = peer_u[l].rearrange("(i j) d -> j i d", j=128)
                v3 = peer_v[l].rearrange("(i j) d -> j i d", j=128)
                for j in range(128):
                    us, usk = ust[j % 2], "ust%d" % (j % 2)
                    vs, vsk = vst_[j % 2], "vstC%d" % (j % 2)
                    ub, ubk = utb[j % 2], "utb%d" % (j % 2)
                    vb, vbk = vbb[j % 2], "vbb%d" % (j % 2)
                    P.dma(lambda e, us=us, j=j: e.dma_start(out=us[:], in_=u3[j]), writes=[usk])
                    P.dma(lambda e, vs=vs, j=j: e.dma_start(out=vs[:], in_=v3[j]), writes=[vsk], q="act")
                    pa, pak = psr.next()
                    pb, pbk = psr.next()
                    for c in range(8):
                        pp_ = pa if c < 4 else pb
                        ppk = pak if c < 4 else pbk
                        P.pe(lambda e, pp_=pp_, c=c, us=us: e.transpose(out=pp_[:, (c % 4) * 128:(c % 4 + 1) * 128], in_=us[:, c * 128:(c + 1) * 128], identity=identf[:]),
                             reads=[usk, "identf"], writes=[ppk])
                    P.dve(lambda e, pa=pa, ub=ub: e.tensor_tensor(out=ub[:, 0:4, :], in0=pa[:].rearrange("p (c i) -> p c i", c=4), in1=gff[:, 0:4, :].to_broadcast([128, 4, 128]), op=ALU.mult),
                          reads=[pak, "gff"], writes=[ubk])
                    P.dve(lambda e, pb=pb, ub=ub: e.tensor_tensor(out=ub[:, 4:8, :], in0=pb[:].rearrange("p (c i) -> p c i", c=4), in1=gff[:, 4:8, :].to_broadcast([128, 4, 128]), op=ALU.mult),
                          reads=[pbk, "gff"], writes=[ubk])
                    P.pool(lambda e, vs=vs, vb=vb: e.tensor_copy(out=vb[:], in_=vs[:]), reads=[vsk], writes=[vbk])
                    P.dma(lambda e, ub=ub, j=j: e.dma_start(out=UTd[j], in_=ub[:]), reads=[ubk], q="pool")
                    P.dma(lambda e, vb=vb, j=j: e.dma_start(out=VBd[j], in_=vb[:]), reads=[vbk], q="pool")
                hin = [sb("hinC%d" % i, [128, 8, 512]) for i in range(2)]
                yl = sb("ylC", [128, 4, 512])
                ya = sb("yaC", [128, 4, 512])
                ysq = sb("ysqC", [128, 8, 512], BF16)
                ylb = sb("ylbC", [128, 4, 512], BF16)
                yab = sb("yabC", [128, 4, 512], BF16)
                rt = sb("rtC", [128, 512])
                rl = sb("rlC", [128, 512])
                ra = sb("raC", [128, 512])
                rf = sb("rfC", [128, 512])
                hnf = sb("hnfC", [128, 8, 512], BF16)
                pqo = [sb("pqo%d" % i, [128, 512], BF16) for i in range(3)]
                pqr = Rot(pqo, "pqo")
                for ti, (p0, w) in enumerate(tiles):
                    hi, hik = hin[ti % 2], "hinC%d" % (ti % 2)
                    P.dma(lambda e, hi=hi, p0=p0, w=w: e.dma_start(out=hi[:, :, 0:w], in_=hT3(hT)[:, :, p0:p0 + w]), writes=[hik])
                    P.dma(lambda e, p0=p0, w=w: e.dma_start(out=yl[:, :, 0:w], in_=hT3(ylT)[:, :, p0:p0 + w]), writes=["ylC"])
                    P.dma(lambda e, p0=p0, w=w: e.dma_start(out=ya[:, :, 0:w], in_=hT3(yaT)[:, :, p0:p0 + w]), writes=["yaC"], q="act")
                    P.act(lambda e, w=w: e.activation(out=ysq[:, 0:4, 0:w], in_=yl[:, :, 0:w], func=AF.Square), reads=["ylC"], writes=["ysqC"])
                    ps_, psk = psr.next()
                    rms_rstd(P, ps_, psk, lambda c, w=w: ysq[:, c, 0:w], "ysqC", 4, w, 1.0 / 512, rt, "rtC", rl, "rlC")
                    P.dve(lambda e, w=w: e.tensor_tensor(out=ylb[:, :, 0:w], in0=yl[:, :, 0:w], in1=rl[:, 0:w].unsqueeze(1).to_broadcast([128, 4, w]), op=ALU.mult),
                          reads=["ylC", "rlC"], writes=["ylbC"])
                    P.act(lambda e, w=w: e.activation(out=ysq[:, 4:8, 0:w], in_=ya[:, :, 0:w], func=AF.Square), reads=["yaC"], writes=["ysqC2"])
                    ps_, psk = psr.next()
                    rms_rstd(P, ps_, psk, lambda c, w=w: ysq[:, 4 + c, 0:w], "ysqC2", 4, w, 1.0 / 512, rt, "rtC", ra, "raC")
                    P.pool(lambda e, w=w: e.tensor_tensor(out=yab[:, :, 0:w], in0=ya[:, :, 0:w], in1=ra[:, 0:w].unsqueeze(1).to_broadcast([128, 4, w]), op=ALU.mult),
                           reads=["yaC", "raC"], writes=["yabC"])
                    for oc in range(8):
                        po, pok = psr.next()
                        for c in range(8):
                            rhs = ylb[:, c, 0:w] if c < 4 else yab[:, c - 4, 0:w]
                            P.pe(lambda e, po=po, c=c, oc=oc, rhs=rhs, w=w: e.matmul(po[:, 0:w], lhsT=woutb[:, c, oc * 128:(oc + 1) * 128], rhs=rhs, start=(c == 0), stop=(c == 7)),
                                 reads=["woutb", "ylbC", "yabC"], writes=[pok])
                        P.dve(lambda e, po=po, hi=hi, oc=oc, w=w: e.tensor_tensor(out=hi[:, oc, 0:w], in0=po[:, 0:w], in1=hi[:, oc, 0:w], op=ALU.add), reads=[pok, hik], writes=[hik])
                    P.dma(lambda e, hi=hi, p0=p0, w=w: e.dma_start(out=hT3(hT)[:, :, p0:p0 + w], in_=hi[:, :, 0:w]), reads=[hik], q="pool")
                    P.act(lambda e, hi=hi, w=w: e.activation(out=ysq[:, :, 0:w], in_=hi[:, :, 0:w], func=AF.Square), reads=[hik], writes=["ysqC", "ysqC2"])
                    ps_, psk = psr.next()
                    rms_rstd(P, ps_, psk, lambda c, w=w: ysq[:, c, 0:w], "ysqC", 8, w, 1.0 / D, rt, "rtC", rf, "rfC")
                    P.dve(lambda e, hi=hi, w=w: e.tensor_tensor(out=hnf[:, :, 0:w], in0=hi[:, :, 0:w], in1=rf[:, 0:w].unsqueeze(1).to_broadcast([128, 8, w]), op=ALU.mult),
                          reads=[hik, "rfC"], writes=["hnfC"])
                    P.dma(lambda e, p0=p0, w=w: e.dma_start(out=hT3(hnTd)[:, :, p0:p0 + w], in_=hnf[:, :, 0:w]), reads=["hnfC"], q="pool")
                    for gp in range(16):
                        po, pok = psr.next()
                        for c in range(8):
                            P.pe(lambda e, po=po, c=c, gp=gp, w=w: e.matmul(po[:, 0:w], lhsT=wqb[:, c, gp * 128:(gp + 1) * 128], rhs=hnf[:, c, 0:w], start=(c == 0), stop=(c == 7)),
                                 reads=["wqb", "hnfC"], writes=[pok])
                        pq_, pqk = pqr.next()
                        if gp % 2 == 0:
                            P.act(lambda e, po=po, pq_=pq_, w=w: e.activation(out=pq_[:, 0:w], in_=po[:, 0:w], func=AF.Copy), reads=[pok], writes=[pqk])
                        else:
                            P.dve(lambda e, po=po, pq_=pq_, w=w: e.tensor_copy(out=pq_[:, 0:w], in_=po[:, 0:w]), reads=[pok], writes=[pqk])
                        P.dma(lambda e, pq_=pq_, gp=gp, p0=p0, w=w: e.dma_start(out=pqTd[gp, :, p0:p0 + w], in_=pq_[:, 0:w]), reads=[pqk])
                P.emit()

            with ExitStack() as es:
                P = Prog(sy)
                sb = lambda n, s, d=F32: es.enter_context(SBT(n, s, d))
                pg = [es.enter_context(PST("pDg%d" % i, [128, 4, 128], F32)) for i in range(2)]
                ptb = [es.enter_context(PST("pDt%d" % i, [128, 8, 128], BF16)) for i in range(2)]
                pacc = [es.enter_context(PST("pDo%d" % i, [128, 512], F32)) for i in range(2)]
                pgr, ptr_ = Rot(pg, "pDg"), Rot(ptb, "pDt")
                pab = [es.enter_context(PST("pDa%d" % i, [128, 4, 128], F32)) for i in range(2)]
                par = Rot([pab[i][:, 0, :] for i in range(2)], "pDa")
                skf = sb("skf", [128, 2, 128])
                skT = sb("skT", [128, 2, 128], BF16)
                P.dma(lambda e: e.dma_start(out=skf[:], in_=peer_sk[l].rearrange("p k d -> k p d")), writes=["skf"])
                pg0, pg0k = pgr.next()
                for p_ in range(2):
                    P.pe(lambda e, p_=p_: e.transpose(out=pg0[:, p_, :], in_=skf[:, p_, :], identity=identf[:]), reads=["skf", "identf"], writes=[pg0k])
                P.dve(lambda e: e.tensor_copy(out=skT[:], in_=pg0[:, 0:2, :]), reads=[pg0k], writes=["skT"])
                hnb2 = [sb("hnbD%d" % i, [128, 8, 128], BF16) for i in range(2)]
                qTb = sb("qTD", [128, 16, 128], BF16)
                s_sb = sb("s_sb", [128, 16, 128])
                scrD = sb("scrD", [128, D])
                wk2 = [scrD[:, i * 256:(i + 1) * 256] for i in range(2)]
                top = sb("topD", [128, 16, 16])
                cand2 = [scrD[:, 512 + i * 256:512 + (i + 1) * 256] for i in range(2)]
                best = sb("bestD", [128, 8, 16])
                ebst = sb("ebstD", [128, 8, 16])
                zz = sb("zzD", [128, 8, 1])
                thr = sb("thrD", [128, 8, 16])
                e1 = sb("e1D", [128, 8, 16])
                c1 = sb("c1D", [128, 8, 1])
                e2 = sb("e2D", [128, 8, 128])
                tm = sb("tmD", [128, 64, 128], BF16)
                a1t = sb("a1tD", [128, 128, 128], BF16)
                rtt = sb("rttD", [128, 128, 128], BF16)
                gs2 = [sb("gsD%d" % i, [128, 128, 128], BF16) for i in range(2)]
                NWB = 6
                utj = [sb("utj%d" % i, [128, 8, 128], BF16) for i in range(NWB)]
                vbj = [sb("vbj%d" % i, [128, D], BF16) for i in range(NWB)]
                gat = [sb("gat%d" % i, [128, 128]) for i in range(3)]
                wT = [sb("wT%d" % i, [128, 128], BF16) for i in range(3)]
                osb = scrD
                hin = sb("hinD", [128, 8, 128])
                OSBK = ["wkD0", "wkD1", "candD0", "candD1", "hinD"]
                utr, vbr, gar, wtr = Rot(utj, "utj"), Rot(vbj, "vbj"), Rot(gat, "gat"), Rot(wT, "wT")
                wkr, cdr = Rot(wk2, "wkD"), Rot(cand2, "candD")
                s4 = s_sb[:].rearrange("t (h q) k -> t h q k", q=2)
                top4 = top[:].rearrange("t (h q) k -> t h q k", q=2)
                tmx = tm[:].rearrange("t x (h a) -> t x h a", h=8)
                evi = [0]

                def evac(out_ap, in_ap, rk, wk):
                    evi[0] += 1
                    if evi[0] % 2 == 0:
                        P.act(lambda e: e.activation(out=out_ap, in_=in_ap, func=AF.Copy), reads=rk, writes=wk)
                    else:
                        P.dve(lambda e: e.tensor_copy(out=out_ap, in_=in_ap), reads=rk, writes=wk)

                def bx(ap3, hx):
                    return ap3[:, :, hx * 64:(hx + 1) * 64].rearrange("t h x -> t x h").unsqueeze(3).to_broadcast([128, 64, 8, 16])

                def ba(ap3):
                    return ap3.unsqueeze(1).to_broadcast([128, 64, 8, 16])

                def cmp_a1(hx):
                    P.dve(lambda e: e.tensor_tensor(out=tmx, in0=bx(s4[:, :, 0, :], hx), in1=ba(top4[:, :, 0, :]), op=ALU.is_equal), reads=["s_sb", "topD"], writes=["tmD"])

                def stage1(b):
                    p0 = b * 128
                    P.dma(lambda e: e.dma_start(out=qTb[:], in_=pqTd[:, :, p0:p0 + 128].rearrange("g d t -> d g t")), writes=["qTD"], q="pool")
                    for g4 in range(4):
                        ps, psk = pgr.next()
                        for k in range(4):
                            gp = g4 * 4 + k
                            P.pe(lambda e, ps=ps, k=k, gp=gp: e.matmul(ps[:, k, :], lhsT=qTb[:, gp, :], rhs=skT[:, gp % 2, :], start=True, stop=True),
                                 reads=["qTD", "skT"], writes=[psk])
                        evac(s_sb[:, g4 * 4:(g4 + 1) * 4, :], ps[:], [psk], ["s_sb"])
                    for gp in range(16):
                        wk_, wkk = wkr.next()
                        P.dve(lambda e, gp=gp: e.max(out=top[:, gp, 0:8], in_=s_sb[:, gp, :]), reads=["s_sb"], writes=["topD"])
                        P.dve(lambda e, gp=gp, wk_=wk_: e.match_replace(out=wk_[:, 0:128], in_to_replace=top[:, gp, 0:8], in_values=s_sb[:, gp, :], imm_value=-1e30),
                              reads=["s_sb", "topD"], writes=[wkk])
                        P.dve(lambda e, gp=gp, wk_=wk_: e.max(out=top[:, gp, 8:16], in_=wk_[:, 0:128]), reads=[wkk], writes=["topD"])
                    for h in range(8):
                        cd, cdk = cdr.next()
                        wk_, wkk = wkr.next()
                        P.dve(lambda e, h=h, cd=cd: e.tensor_tensor(out=cd[:].rearrange("t (a b) -> t a b", a=16), in0=top4[:, h, 0, :].unsqueeze(2).to_broadcast([128, 16, 16]),
                                                                    in1=top4[:, h, 1, :].unsqueeze(1).to_broadcast([128, 16, 16]), op=ALU.add), reads=["topD"], writes=[cdk])
                        P.dve(lambda e, h=h, cd=cd: e.max(out=best[:, h, 0:8], in_=cd[:]), reads=[cdk], writes=["bestD"])
                        P.dve(lambda e, h=h, cd=cd, wk_=wk_: e.match_replace(out=wk_[:], in_to_replace=best[:, h, 0:8], in_values=cd[:], imm_value=-1e30),
                              reads=[cdk, "bestD"], writes=[wkk])
                        P.dve(lambda e, h=h, wk_=wk_: e.max(out=best[:, h, 8:16], in_=wk_[:]), reads=[wkk], writes=["bestD"])
                    P.dve(lambda e: e.tensor_tensor(out=ebst[:], in0=best[:], in1=best[:, :, 0:1].to_broadcast([128, 8, 16]), op=ALU.subtract), reads=["bestD"], writes=["ebstD"])
                    P.act(lambda e: e.activation(out=ebst[:], in_=ebst[:], func=AF.Exp), reads=["ebstD"], writes=["ebstD"])
                    P.dve(lambda e: e.reduce_sum(out=zz[:, :, 0], in_=ebst[:], axis=AX.X), reads=["ebstD"], writes=["zzD"])
                    P.dve(lambda e: e.reciprocal(out=zz[:], in_=zz[:]), reads=["zzD"], writes=["zzD"])
                    P.dve(lambda e: e.tensor_scalar(out=c1[:], in0=best[:, :, 15:16], scalar1=-2e-5, scalar2=None, op0=ALU.add), reads=["bestD"], writes=["c1D"])
                    P.dve(lambda e: e.tensor_tensor(out=thr[:], in0=c1[:].to_broadcast([128, 8, 16]), in1=top4[:, :, 0, :], op=ALU.subtract), reads=["c1D", "topD"], writes=["thrD"])
                    P.dve(lambda e: e.tensor_tensor(out=c1[:], in0=top4[:, :, 1, 0:1], in1=best[:, :, 0:1], op=ALU.subtract), reads=["bestD", "topD", "thrD"], writes=["c1D"])
                    P.dve(lambda e: e.tensor_tensor(out=e1[:], in0=top4[:, :, 0, :], in1=c1[:].to_broadcast([128, 8, 16]), op=ALU.add), reads=["c1D", "topD"], writes=["e1D"])
                    P.act(lambda e: e.activation(out=e1[:], in_=e1[:], func=AF.Exp), reads=["e1D"], writes=["e1D"])
                    P.dve(lambda e: e.tensor_tensor(out=e1[:], in0=e1[:], in1=zz[:].to_broadcast([128, 8, 16]), op=ALU.mult), reads=["e1D", "zzD"], writes=["e1D"])
                    P.dve(lambda e: e.tensor_tensor(out=e2[:], in0=s4[:, :, 1, :], in1=top4[:, :, 1, 0:1].to_broadcast([128, 8, 128]), op=ALU.subtract), reads=["s_sb", "topD"], writes=["e2D"])
                    P.act(lambda e: e.activation(out=e2[:], in_=e2[:], func=AF.Exp), reads=["e2D"], writes=["e2D"])
                    cmp_a1(0)

                def stage_tr(dst, dstk, hx):
                    for x8 in range(8):
                        pt_, ptk = ptr_.next()
                        for k in range(8):
                            P.pe(lambda e, pt_=pt_, k=k, x8=x8: e.transpose(out=pt_[:, k, :], in_=tm[:, x8 * 8 + k, :], identity=identb[:]), reads=["tmD", "identb"], writes=[ptk])
                        xo = hx * 64 + x8 * 8
                        evac(dst[:, :, xo:xo + 8].rearrange("p t x -> p x t"), pt_[:], [ptk], [dstk])

                def stage_r(hx):
                    P.dve(lambda e: e.tensor_tensor(out=tmx, in0=bx(s4[:, :, 1, :], hx), in1=ba(thr[:]), op=ALU.is_ge), reads=["s_sb", "thrD"], writes=["tmD"])
                    P.dve(lambda e: e.tensor_tensor(out=tmx, in0=tmx, in1=bx(e2[:], hx), op=ALU.mult), reads=["tmD", "e2D"], writes=["tmD"])
                    P.dve(lambda e: e.tensor_tensor(out=tmx, in0=tmx, in1=ba(e1[:]), op=ALU.mult), reads=["tmD", "e1D"], writes=["tmD"])

                def stage_g(b):
                    gs, gsk = gs2[b % 2], "gsD%d" % (b % 2)
                    for t4 in range(32):
                        pgt, pgk = pgr.next()
                        for k in range(4):
                            t_ = t4 * 4 + k
                            P.pe(lambda e, pgt=pgt, k=k, t_=t_: e.matmul(pgt[:, k, :], lhsT=a1t[:, t_, :], rhs=rtt[:, t_, :], start=True, stop=True), reads=["a1tD", "rttD"], writes=[pgk])
                        evac(gs[:, :, t4 * 4:(t4 + 1) * 4].rearrange("p j t -> p t j"), pgt[:], [pgk], [gsk])

                def load_block(b):
                    p0 = b * 128
                    hn, hnk = hnb2[b % 2], "hnbD%d" % (b % 2)
                    P.dma(lambda e: e.dma_start(out=hn[:], in_=hT3(hnTd)[:, :, p0:p0 + 128]), writes=[hnk], q="pool")

                pend = {}

                def main_p1(b, j):
                    hn, hnk = hnb2[b % 2], "hnbD%d" % (b % 2)
                    gs, gsk = gs2[b % 2], "gsD%d" % (b % 2)
                    ut, utk = utr.next()
                    vb, vbk = vbr.next()
                    P.dma(lambda e: e.dma_start(out=ut[:], in_=UTd[j]), writes=[utk])
                    P.dma(lambda e: e.dma_start(out=vb[:], in_=VBd[j]), writes=[vbk], q=("act" if j % 2 else "sp"))
                    pa, pak = par.next()
                    for c in range(8):
                        P.pe(lambda e, c=c: e.matmul(pa, lhsT=ut[:, c, :], rhs=hn[:, c, :], start=(c == 0), stop=(c == 7)), reads=[utk, hnk], writes=[pak])
                    ga, gak = gar.next()
                    P.act(lambda e: e.activation(out=ga[:], in_=pa, func=AF.Gelu), reads=[pak], writes=[gak])
                    wt, wtk = wtr.next()
                    P.pool(lambda e: e.tensor_tensor(out=wt[:], in0=ga[:], in1=gs[:, j, :], op=ALU.mult), reads=[gak, gsk], writes=[wtk])
                    pend[(b, j)] = (wt, wtk, vb, vbk)

                def main_p2(b, j):
                    wt, wtk, vb, vbk = pend.pop((b, j))
                    for hf in range(2):
                        P.pe(lambda e, hf=hf: e.matmul(pacc[hf][:, :], lhsT=wt[:], rhs=vb[:, hf * 512:(hf + 1) * 512], start=(j == 0), stop=(j == 127)),
                             reads=[wtk, vbk], writes=["pDo%d" % hf])

                def finish_block(b):
                    p0 = b * 128
                    P.dma(lambda e: e.dma_start(out=hin[:], in_=hT3(hT)[:, :, p0:p0 + 128]), writes=["hinD"], q="pool")
                    P.act(lambda e: e.activation(out=osb[:, 0:512], in_=pacc[0][:, :], func=AF.Copy), reads=["pDo0"], writes=OSBK)
                    P.dve(lambda e: e.tensor_copy(out=osb[:, 512:1024], in_=pacc[1][:, :]), reads=["pDo1"], writes=OSBK)
                    for hf in range(2):
                        pgt, pgk = pgr.next()
                        for k in range(4):
                            c = hf * 4 + k
                            P.pe(lambda e, pgt=pgt, k=k, c=c: e.transpose(out=pgt[:, k, :], in_=osb[:, c * 128:(c + 1) * 128], identity=identf[:]), reads=OSBK + ["identf"], writes=[pgk])
                        P.dve(lambda e, pgt=pgt, hf=hf: e.tensor_tensor(out=hin[:, hf * 4:(hf + 1) * 4, :], in0=pgt[:], in1=hin[:, hf * 4:(hf + 1) * 4, :], op=ALU.add), reads=[pgk, "hinD"], writes=["hinD"])
                    P.dma(lambda e: e.dma_start(out=hT3(hT)[:, :, p0:p0 + 128], in_=hin[:]), reads=["hinD"], q="pool")

                load_block(0)
                stage1(0)
                stage_tr(a1t, "a1tD", 0)
                cmp_a1(1)
                stage_tr(a1t, "a1tD", 1)
                stage_r(0)
                stage_tr(rtt, "rttD", 0)
                stage_r(1)
                stage_tr(rtt, "rttD", 1)
                stage_g(0)
                main_p1(0, 0)
                for b in range(NB):
                    nb_ = b + 1 if b + 1 < NB else None
                    for j in range(128):
                        if nb_ is not None:
                            if j == 0:
                                load_block(nb_)
                                stage1(nb_)
                            elif j == 24:
                                stage_tr(a1t, "a1tD", 0)
                                cmp_a1(1)
                            elif j == 42:
                                stage_tr(a1t, "a1tD", 1)
                                stage_r(0)
                            elif j == 62:
                                stage_tr(rtt, "rttD", 0)
                                stage_r(1)
                            elif j == 82:
                                stage_tr(rtt, "rttD", 1)
                            elif j == 100:
                                stage_g(nb_)
                        if j + 1 < 128:
                            main_p1(b, j + 1)
                        elif nb_ is not None:
                            main_p1(nb_, 0)
                        main_p2(b, j)
                    finish_block(b)
                P.emit()

        sfx[0] = "_F"
        with ExitStack() as es:
            P = Prog(sy)
            sb = lambda n, s, d=F32: es.enter_context(SBT(n, s, d))
            hin = [sb("hinF%d" % i, [128, 8, 128]) for i in range(2)]
            xo = [sb("xoF%d" % i, [128, D]) for i in range(2)]
            pt = [es.enter_context(PST("pFt%d" % i, [128, 4, 128], F32)) for i in range(4)]
            for b in range(1, NB):
                hi, hik = hin[b % 2], "hinF%d" % (b % 2)
                xoo, xok = xo[b % 2], "xoF%d" % (b % 2)
                P.dma(lambda e, hi=hi, b=b: e.dma_start(out=hi[:], in_=hT3(hT)[:, :, b * 128:(b + 1) * 128]), writes=[hik])
                for hf in range(2):
                    ptt, ptk = pt[(2 * b + hf) % 4], "pFt%d" % ((2 * b + hf) % 4)
                    for c in range(4):
                        P.pe(lambda e, ptt=ptt, c=c, hf=hf, hi=hi: e.transpose(out=ptt[:, c, :], in_=hi[:, hf * 4 + c, :], identity=identf[:]), reads=[hik, "identf"], writes=[ptk])
                    if hf == 0:
                        P.dve(lambda e, ptt=ptt, xoo=xoo: e.tensor_copy(out=xoo[:, 0:512], in_=ptt[:].rearrange("p c t -> p (c t)")), reads=[ptk], writes=[xok])
                    else:
                        P.act(lambda e, ptt=ptt, xoo=xoo: e.activation(out=xoo[:, 512:1024], in_=ptt[:].rearrange("p c t -> p (c t)"), func=AF.Copy), reads=[ptk], writes=[xok])
                P.dma(lambda e, xoo=xoo, b=b: e.dma_start(out=out[(b - 1) * 128:b * 128, :], in_=xoo[:]), reads=[xok], q="pool")
            P.emit()
    return nc


def host_consts(NB):
    NP = NB * 128
    ident = np.eye(128, dtype=np.float32)
    NEG = -30000.0
    mask = np.zeros((4, 128, 512), dtype=np.float32)
    k = np.arange(128)[:, None]
    for j in range(4):
        q = np.arange(512)[None, :]
        qb = q // 128
        qi = q % 128
        m = np.where(qb < j, NEG, np.where(qb == j, np.where(k > qi, NEG, 0.0), 0.0))
        mask[j] = m
    half = 16
    freq = (np.float32(10000.0) ** (-np.arange(half, dtype=np.float32) / np.float32(half))).astype(np.float32)
    pos = (np.arange(NP, dtype=np.float32) - np.float32(112.0)).astype(np.float32)
    ang = (pos[None, :] * freq[:, None]).astype(np.float32)
    C = np.ones((96, NP), dtype=np.float32)
    S = np.zeros((96, NP), dtype=np.float32)
    C[64:80] = np.cos(ang)
    C[80:96] = np.cos(ang)
    S[64:80] = np.sin(ang)
    S[80:96] = np.sin(ang)
    rot = np.zeros((96, 96), dtype=np.float32)
    for i in range(16):
        rot[80 + i, 64 + i] = -1.0
        rot[64 + i, 80 + i] = 1.0
    padb = np.zeros((128, 1), dtype=np.float32)
    padb[:112] = NEG
    return {"c_ident": ident, "c_mask": mask, "c_ropeC": C, "c_ropeS": S, "c_rot": rot, "c_padb": padb}


def make_in_maps(inputs, NB, depth, n_cores=8):
    consts = host_consts(NB)
    B = inputs["x"].shape[0]
    maps = []
    for c in range(n_cores):
        b = c % B
        m = {"x": np.ascontiguousarray(inputs["x"][b])}
        for k, v in inputs.items():
            if k == "x":
                continue
            v = np.asarray(v)
            if k in ("lru_ba", "lru_bx"):
                v = v.reshape(v.shape[0], 512)
            if k != "meta_tokens":
                v = v[:depth]
            m[k] = np.ascontiguousarray(v, dtype=np.float32)
        m.update(consts)
        maps.append(m)
    return maps


def kernel(**inputs):
    inputs = {k: np.asarray(v) for k, v in inputs.items()}
    B, SEQ, _ = inputs["x"].shape
    NB = SEQ // 128 + 1
    depth = inputs["w_in"].shape[0]
    nc = build(NB, depth)
    maps = make_in_maps(inputs, NB, depth, 8)
    res = run_bass_kernel_spmd(nc, maps, core_ids=list(range(8)))
    outs = [np.asarray(res.results[b]["out"], dtype=np.float32) for b in range(B)]
    return np.stack(outs, axis=0)
```

```python
import numpy as np
import concourse.bass as bass
import concourse.mybir as mybir
from concourse.bass_utils import run_bass_kernel_spmd
from contextlib import ExitStack

F32 = mybir.dt.float32
BF16 = mybir.dt.bfloat16
AF = mybir.ActivationFunctionType
ALU = mybir.AluOpType
AX = mybir.AxisListType

D = 1024
DIN = 1696
EPS = 1e-6
COMPUTE = ("pe", "act", "dve", "pool")
NDMASEM = {"sp": 24, "pool": 8, "act": 8}


class Op:
    __slots__ = ("eng", "fn", "deps", "is_dma", "sig", "sigval", "dsem", "prevwait")

    def __init__(self, eng, fn, is_dma):
        self.eng = eng
        self.fn = fn
        self.is_dma = is_dma
        self.deps = []
        self.sig = False
        self.sigval = 0
        self.dsem = None
        self.prevwait = None


class Sync:
    def __init__(self, nc, es):
        self.nc = nc
        self.csem = {e: es.enter_context(nc.semaphore("c_" + e)) for e in COMPUTE}
        self.dsem = {}
        for q, n in NDMASEM.items():
            for i in range(n):
                self.dsem[(q, i)] = es.enter_context(nc.semaphore("d_%s%d" % (q, i)))
        self.cnt = {e: 0 for e in COMPUTE}
        self.dcnt = {k: 0 for k in self.dsem}
        self.dma_rr = {q: 0 for q in NDMASEM}
        self.waited = {e: {} for e in ("pe", "act", "dve", "pool", "sp")}


class Prog:
    def __init__(self, sy):
        self.sy = sy
        self.nc = sy.nc
        self.ops = []
        self.streams = {"pe": [], "act": [], "dve": [], "pool": [], "sp": []}
        self.lastw = {}
        self.readers = {}

    def op(self, eng, fn, reads=(), writes=(), dma=False):
        o = Op(eng, fn, dma)
        deps = set()
        for r in reads:
            w = self.lastw.get(r)
            if w is not None:
                deps.add(w)
        for r in writes:
            w = self.lastw.get(r)
            if w is not None:
                deps.add(w)
            for rd in self.readers.get(r, ()):
                deps.add(rd)
        fl = []
        for d in deps:
            if (not d.is_dma) and (not dma) and d.eng == eng and eng == "pe":
                continue
            fl.append(d)
        o.deps = fl
        for r in reads:
            self.readers.setdefault(r, []).append(o)
        for r in writes:
            self.lastw[r] = o
            self.readers[r] = []
        if dma:
            sy = self.sy
            k = sy.dma_rr[eng]
            sy.dma_rr[eng] = k + 1
            slot = (eng, k % NDMASEM[eng])
            prev = sy.dcnt[slot]
            sy.dcnt[slot] = prev + 16
            o.dsem = (slot, prev + 16)
            o.prevwait = (slot, prev) if prev else None
        self.ops.append(o)
        self.streams[eng].append(o)
        return o

    def pe(self, fn, reads=(), writes=()):
        return self.op("pe", fn, reads, writes)

    def act(self, fn, reads=(), writes=()):
        return self.op("act", fn, reads, writes)

    def dve(self, fn, reads=(), writes=()):
        return self.op("dve", fn, reads, writes)

    def pool(self, fn, reads=(), writes=()):
        return self.op("pool", fn, reads, writes)

    def dma(self, fn, reads=(), writes=(), q="sp"):
        return self.op(q, fn, reads, writes, dma=True)

    def emit(self):
        nc = self.nc
        sy = self.sy
        for o in self.ops:
            for d in o.deps:
                if not d.is_dma:
                    d.sig = True
        for e in COMPUTE:
            for o in reversed(self.streams[e]):
                if not o.is_dma:
                    o.sig = True
                    break
        for o in self.ops:
            if not o.is_dma and o.sig:
                sy.cnt[o.eng] += 1
                o.sigval = sy.cnt[o.eng]
        fin_c = dict(sy.cnt)
        fin_d = dict(sy.dcnt)
        with nc.Block() as block:
            engmap = {"pe": block.tensor, "act": block.scalar, "dve": block.vector,
                      "pool": block.gpsimd, "sp": block.sync}

            def run_stream(ename, e):
                waited = sy.waited[ename]
                for o in self.streams[ename]:
                    need = {}
                    for d in o.deps:
                        if d.is_dma:
                            key = ("d", d.dsem[0])
                            v = d.dsem[1]
                        else:
                            key = ("c", d.eng)
                            v = d.sigval
                        if waited.get(key, 0) >= v:
                            continue
                        if need.get(key, 0) < v:
                            need[key] = v
                    if o.is_dma and o.prevwait is not None:
                        key = ("d", o.prevwait[0])
                        v = o.prevwait[1]
                        if waited.get(key, 0) < v and need.get(key, 0) < v:
                            need[key] = v
                    for key, v in need.items():
                        s = sy.dsem[key[1]] if key[0] == "d" else sy.csem[key[1]]
                        e.wait_ge(s, v)
                        waited[key] = v
                    ins = o.fn(e)
                    if o.is_dma:
                        ins.then_inc(sy.dsem[o.dsem[0]], 16)
                    elif o.sig:
                        ins.then_inc(sy.csem[o.eng], 1)
                for f in COMPUTE:
                    if f != ename and fin_c[f] > waited.get(("c", f), 0):
                        e.wait_ge(sy.csem[f], fin_c[f])
                        waited[("c", f)] = fin_c[f]
                for slot, v in fin_d.items():
                    if v > waited.get(("d", slot), 0):
                        e.wait_ge(sy.dsem[slot], v)
                        waited[("d", slot)] = v

            for ename in ("sp", "pe", "act", "dve", "pool"):
                def mk(ename):
                    def body(e):
                        run_stream(ename, e)
                    return body
                engmap[ename](mk(ename))


class Rot:
    def __init__(self, tiles, name):
        self.tiles = tiles
        self.name = name
        self.i = 0

    def next(self):
        k = self.i % len(self.tiles)
        self.i += 1
        return self.tiles[k], "%s%d" % (self.name, k)


def build(NB, DEPTH, NE=128):
    NP = NB * 128
    SEQ = NP - 128
    nc = bass.Bass("TRN2", target_bir_lowering=False)
    inp = {}
    sfx = [""]

    def SBT(name, shape, dt):
        return nc.sbuf_tensor(name + sfx[0], shape, dt)

    def PST(name, shape, dt):
        return nc.psum_tensor(name + sfx[0], shape, dt)

    def din(name, shape, dt=F32):
        inp[name] = nc.dram_tensor(name, list(shape), dt, kind="ExternalInput").ap()
        return inp[name]

    x = din("x", [SEQ, D])
    meta = din("meta_tokens", [16, D])
    mix_g = din("mix_norm_g", [DEPTH, D])
    w_in = din("w_in", [DEPTH, D, DIN])
    conv_w = din("conv_w", [DEPTH, 4, 512])
    conv_b = din("conv_b", [DEPTH, 512])
    lru_wa = din("lru_wa", [DEPTH, 8, 64, 64])
    lru_ba = din("lru_ba", [DEPTH, 512])
    lru_wx = din("lru_wx", [DEPTH, 8, 64, 64])
    lru_bx = din("lru_bx", [DEPTH, 512])
    lru_lam = din("lru_lambda", [DEPTH, 512])
    qlg = din("q_lora_norm_g", [DEPTH, 384])
    w_uq = din("w_uq", [DEPTH, 384, 768])
    kvlg = din("kv_lora_norm_g", [DEPTH, 256])
    w_ukv = din("w_ukv", [DEPTH, 256, 1024])
    qhg = din("q_head_norm_g", [DEPTH, 96])
    khg = din("k_head_norm_g", [DEPTH, 96])
    log_ = din("lru_out_norm_g", [DEPTH, 512])
    aog = din("attn_out_norm_g", [DEPTH, 512])
    w_out = din("w_out", [DEPTH, D, D])
    ffn_g = din("ffn_norm_g", [DEPTH, D])
    peer_wq = din("peer_wq", [DEPTH, D, 2048])
    peer_sk = din("peer_subkeys", [DEPTH, 2, 128, 128])
    peer_u = din("peer_u", [DEPTH, 16384, D])
    peer_v = din("peer_v", [DEPTH, 16384, D])
    c_ident = din("c_ident", [128, 128])
    c_mask = din("c_mask", [4, 128, 512])
    c_ropeC = din("c_ropeC", [96, NP])
    c_ropeS = din("c_ropeS", [96, NP])
    c_rot = din("c_rot", [96, 96])
    c_padb = din("c_padb", [128, 1])
    out = nc.dram_tensor("out", [SEQ, D], F32, kind="ExternalOutput").ap()

    def scr(name, shape, dt):
        return nc.dram_tensor(name, list(shape), dt, kind="Internal").ap()

    hT = scr("hT", [D, NP], F32)
    ylT = scr("ylT", [512, NP], F32)
    yaT = scr("yaT", [512, NP], F32)
    qTd = scr("qTd", [8, 96, NP], BF16)
    kTd = scr("kTd", [8, 96, NP], BF16)
    vvd = scr("vvd", [NP, 8, 64], BF16)
    hnTd = scr("hnTd", [D, NP], BF16)
    pqTd = scr("pqTd", [16, 128, NP], BF16)
    UTd = scr("UTd", [128, 128, 8, 128], BF16)
    VBd = scr("VBd", [128, 128, D], BF16)

    tiles = []
    p = 0
    while p < NP:
        w = min(512, NP - p)
        tiles.append((p, w))
        p += w

    es0 = ExitStack()
    with es0:
        sy = Sync(nc, es0)
        identf = es0.enter_context(SBT("identf", [128, 128], F32))
        identb = es0.enter_context(SBT("identb", [128, 128], BF16))
        onesb = es0.enter_context(SBT("onesb", [128, 128], BF16))
        onesf = es0.enter_context(SBT("onesf", [128, 128], F32))
        epsb = es0.enter_context(SBT("epsb", [128, 1], F32))
        padb = es0.enter_context(SBT("padb", [128, 1], F32))

        def hT3(ap):
            return ap.rearrange("(c p) n -> p c n", p=128)

        with ExitStack() as es:
            P = Prog(sy)
            sb = lambda n, s, d=F32: es.enter_context(SBT(n, s, d))
            xin = [sb("xin%d" % i, [128, D]) for i in range(2)]
            xo = [sb("xo%d" % i, [128, 8, 128]) for i in range(2)]
            pt = [es.enter_context(PST("p0t%d" % i, [128, 4, 128], F32)) for i in range(4)]
            P.dma(lambda e: e.dma_start(out=identf[:], in_=c_ident[:, :]), writes=["identf"])
            P.dma(lambda e: e.dma_start(out=padb[:], in_=c_padb[:, :]), writes=["padb"])
            P.dve(lambda e: e.tensor_copy(out=identb[:], in_=identf[:]), reads=["identf"], writes=["identb"])
            P.pool(lambda e: e.memset(onesb[:], 1.0), writes=["onesb"])
            P.pool(lambda e: e.memset(onesf[:], 1.0), writes=["onesf"])
            P.pool(lambda e: e.memset(epsb[:], EPS), writes=["epsb"])
            for b in range(NB):
                xi, xk = xin[b % 2], "xin%d" % (b % 2)
                xoo, xok = xo[b % 2], "xo%d" % (b % 2)
                if b == 0:
                    P.pool(lambda e, xi=xi: e.memset(xi[:], 0.0), writes=[xk])
                    P.dma(lambda e, xi=xi: e.dma_start(out=xi[112:128, :], in_=meta[:, :]), writes=[xk])
                else:
                    P.dma(lambda e, xi=xi, b=b: e.dma_start(out=xi[:], in_=x[(b - 1) * 128:b * 128, :]), writes=[xk])
                for hf in range(2):
                    ptt, ptk = pt[(2 * b + hf) % 4], "p0t%d" % ((2 * b + hf) % 4)
                    for c in range(4):
                        cc = hf * 4 + c
                        P.pe(lambda e, ptt=ptt, c=c, cc=cc, xi=xi: e.transpose(out=ptt[:, c, :], in_=xi[:, cc * 128:(cc + 1) * 128], identity=identf[:]),
                             reads=[xk, "identf"], writes=[ptk])
                    if hf == 0:
                        P.dve(lambda e, ptt=ptt, xoo=xoo: e.tensor_copy(out=xoo[:, 0:4, :], in_=ptt[:]), reads=[ptk], writes=[xok])
                    else:
                        P.act(lambda e, ptt=ptt, xoo=xoo: e.activation(out=xoo[:, 4:8, :], in_=ptt[:], func=AF.Copy), reads=[ptk], writes=[xok])
                P.dma(lambda e, xoo=xoo, b=b: e.dma_start(out=hT3(hT)[:, :, b * 128:(b + 1) * 128], in_=xoo[:]), reads=[xok], q="pool")
            P.emit()

        def colvec_load(P, es, name, src2d, rows, n, ptile, ptk):
            pp = min(n, 128)
            ncol = n // pp
            st = es.enter_context(SBT(name + "_st", [rows, n], F32))
            res = es.enter_context(SBT(name, [pp, ncol, rows], F32))
            P.dma(lambda e: e.dma_start(out=st[:], in_=src2d), writes=[name + "_st"])
            for c in range(ncol):
                P.pe(lambda e, c=c: e.transpose(out=ptile[0:pp, c, 0:rows], in_=st[0:rows, c * pp:(c + 1) * pp], identity=identf[0:rows, 0:rows]),
                     reads=[name + "_st", "identf"], writes=[ptk])
            P.dve(lambda e: e.tensor_copy(out=res[:], in_=ptile[0:pp, 0:ncol, 0:rows]), reads=[ptk], writes=[name])
            return res

        def rms_rstd(P, ps_t, psk, sq_chunks, sqk, nch, w, inv_n, rt, rtk, rr, rrk, parts=128):
            for c in range(nch):
                P.pe(lambda e, c=c: e.matmul(ps_t[0:parts, 0:w], lhsT=onesb[:, 0:parts], rhs=sq_chunks(c), start=(c == 0), stop=(c == nch - 1)),
                     reads=[sqk, "onesb"], writes=[psk])
            P.act(lambda e: e.activation(out=rt[0:parts, 0:w], in_=ps_t[0:parts, 0:w], func=AF.Sqrt, scale=inv_n, bias=epsb[0:parts, :]),
                  reads=[psk, "epsb"], writes=[rtk])
            P.dve(lambda e: e.reciprocal(out=rr[0:parts, 0:w], in_=rt[0:parts, 0:w]), reads=[rtk], writes=[rrk])

        for l in range(DEPTH):
            sfx[0] = "_L%d" % l
            with ExitStack() as es:
                P = Prog(sy)
                sb = lambda n, s, d=F32: es.enter_context(SBT(n, s, d))
                psum = [es.enter_context(PST("pA%d" % i, [128, 512], F32)) for i in range(7)]
                psr = Rot(psum[0:7], "pA")
                pvec = es.enter_context(PST("pAv", [128, 8, 8], F32))
                gm = colvec_load(P, es, "gm", mix_g[l:l + 1, :], 1, D, pvec, "pAv")
                v512 = sb("v512", [128, 4, 8])
                st512 = sb("st512", [8, 512])
                P.dma(lambda e: e.dma_start(out=st512[0:4, :], in_=conv_w[l, :, :]), writes=["st512"])
                for r_, src in ((4, conv_b), (5, lru_ba), (6, lru_bx), (7, lru_lam)):
                    P.dma(lambda e, r_=r_, src=src: e.dma_start(out=st512[r_:r_ + 1, :], in_=src[l:l + 1, :]), writes=["st512"])
                for g in range(4):
                    P.pe(lambda e, g=g: e.transpose(out=pvec[:, g, 0:8], in_=st512[0:8, g * 128:(g + 1) * 128], identity=identf[0:8, 0:8]),
                         reads=["st512", "identf", "gm"], writes=["pAv"])
                P.dve(lambda e: e.tensor_copy(out=v512[:], in_=pvec[:, 0:4, 0:8]), reads=["pAv"], writes=["v512"])
                spt = sb("spt", [128, 4, 1])
                nc8 = sb("nc8", [128, 4, 1])
                nc16 = sb("nc16", [128, 4, 1])
                P.act(lambda e: e.activation(out=spt[:], in_=v512[:, :, 7:8], func=AF.Exp, scale=-1.0), reads=["v512"], writes=["spt"])
                P.act(lambda e: e.activation(out=spt[:], in_=spt[:], func=AF.Ln, bias=onesf[:, 0:1]), reads=["spt", "onesf"], writes=["spt"])
                P.dve(lambda e: e.tensor_scalar(out=nc8[:], in0=spt[:], scalar1=-8.0, scalar2=None, op0=ALU.mult), reads=["spt"], writes=["nc8"])
                P.dve(lambda e: e.tensor_scalar(out=nc16[:], in0=spt[:], scalar1=-16.0, scalar2=None, op0=ALU.mult), reads=["spt"], writes=["nc16"])
                gq = colvec_load(P, es, "gq", qlg[l:l + 1, :], 1, 384, pvec, "pAv")
                gkv = colvec_load(P, es, "gkv", kvlg[l:l + 1, :], 1, 256, pvec, "pAv")
                gqh = colvec_load(P, es, "gqh", qhg[l:l + 1, :], 1, 96, pvec, "pAv")
                gkh = colvec_load(P, es, "gkh", khg[l:l + 1, :], 1, 96, pvec, "pAv")
                wst = [sb("wst%d" % i, [128, DIN]) for i in range(2)]
                winb = sb("winb", [128, 8, 1760], BF16)
                P.pool(lambda e: e.memset(winb[:, :, 1664:1728], 0.0), writes=["winb"])
                for c in range(8):
                    ws, wk = wst[c % 2], "wst%d" % (c % 2)
                    P.dma(lambda e, ws=ws, c=c: e.dma_start(out=ws[:], in_=w_in[l, c * 128:(c + 1) * 128, :]), writes=[wk])
                    P.dve(lambda e, ws=ws, c=c: e.tensor_scalar(out=winb[:, c, 0:1664], in0=ws[:, 0:1664], scalar1=gm[:, c, :], scalar2=None, op0=ALU.mult),
                          reads=[wk, "gm"], writes=["winb"])
                    P.dve(lambda e, ws=ws, c=c: e.tensor_scalar(out=winb[:, c, 1728:1760], in0=ws[:, 1664:1696], scalar1=gm[:, c, :], scalar2=None, op0=ALU.mult),
                          reads=[wk, "gm"], writes=["winb"])
                wuqb = sb("wuqb", [128, 3, 768], BF16)
                for c in range(3):
                    ws, wk = wst[c % 2], "wst%d" % (c % 2)
                    P.dma(lambda e, ws=ws, c=c: e.dma_start(out=ws[:, 0:768], in_=w_uq[l, c * 128:(c + 1) * 128, :]), writes=[wk])
                    P.dve(lambda e, ws=ws, c=c: e.tensor_scalar(out=wuqb[:, c, :], in0=ws[:, 0:768], scalar1=gq[:, c, :], scalar2=None, op0=ALU.mult),
                          reads=[wk, "gq"], writes=["wuqb"])
                wukb = sb("wukb", [128, 2, 8, 64], BF16)
                wuvb = sb("wuvb", [128, 2, 8, 64], BF16)
                for c in range(2):
                    ws, wk = wst[(c + 1) % 2], "wst%d" % ((c + 1) % 2)
                    P.dma(lambda e, ws=ws, c=c: e.dma_start(out=ws[:, 0:1024], in_=w_ukv[l, c * 128:(c + 1) * 128, :]), writes=[wk])
                    wv_ = ws[:, 0:1024].rearrange("p (h t d) -> p h t d", h=8, t=2)
                    P.dve(lambda e, wv_=wv_, c=c: e.tensor_scalar(out=wukb[:, c, :, :], in0=wv_[:, :, 0, :], scalar1=gkv[:, c, :], scalar2=None, op0=ALU.mult),
                          reads=[wk, "gkv"], writes=["wukb"])
                    P.dve(lambda e, wv_=wv_, c=c: e.tensor_scalar(out=wuvb[:, c, :, :], in0=wv_[:, :, 1, :], scalar1=gkv[:, c, :], scalar2=None, op0=ALU.mult),
                          reads=[wk, "gkv"], writes=["wuvb"])
                wab = sb("wab", [128, 4, 128], BF16)
                wxb = sb("wxb", [128, 4, 128], BF16)
                wgs = sb("wgs", [128, 2, 4, 128])
                P.pool(lambda e: e.memset(wgs[:], 0.0), writes=["wgs"])
                for t_, src in ((0, lru_wa), (1, lru_wx)):
                    for g in range(4):
                        for hh in range(2):
                            P.dma(lambda e, t_=t_, src=src, g=g, hh=hh: e.dma_start(out=wgs[hh * 64:(hh + 1) * 64, t_, g, hh * 64:(hh + 1) * 64], in_=src[l, 2 * g + hh, :, :]),
                                  writes=["wgs"])
                P.dve(lambda e: e.tensor_copy(out=wab[:], in_=wgs[:, 0, :, :]), reads=["wgs"], writes=["wab"])
                P.dve(lambda e: e.tensor_copy(out=wxb[:], in_=wgs[:, 1, :, :]), reads=["wgs"], writes=["wxb"])
                rotf = sb("rotf", [96, 96])
                P.dma(lambda e: e.dma_start(out=rotf[:], in_=c_rot[:, :]), writes=["rotf"])
                hin = [sb("hin%d" % i, [128, 8, 512]) for i in range(1)]
                hsq = sb("hsq", [128, 8, 512], BF16)
                hnb = sb("hnb", [128, 8, 512], BF16)
                rt = sb("rtA", [128, 512])
                rz = sb("rzA", [128, 512])
                xlb = [sb("xlb%d" % g, [128, 3 + 512]) for g in range(4)]
                zg = sb("zg", [128, 4, 512])
                zq = sb("zq", [128, 3, 512])
                zkv = sb("zkv", [128, 2, 512])
                zkr = sb("zkr", [96, 512])
                state = sb("state", [128, 4, 1])
                P.pool(lambda e: e.memset(state[:], 0.0), writes=["state"])
                for g in range(4):
                    P.pool(lambda e, g=g: e.memset(xlb[g][:, 0:3], 0.0), writes=["xlb%d" % g])
                tA = [sb("tA%d" % i, [128, 512]) for i in range(7)]
                tAb = [sb("tAb%d" % i, [128, 512], BF16) for i in range(1)]
                cqs = sb("cqs", [128, 3, 512], BF16)
                cqb = sb("cqb", [128, 3, 512], BF16)
                cks = sb("cks", [128, 2, 512], BF16)
                ckb = sb("ckb", [128, 2, 512], BF16)
                rq = sb("rq", [128, 512])
                rkv = sb("rkv", [128, 512])
                rkc = sb("rkc", [128, 4, 1])
                ropeC = sb("ropeC", [96, 512])
                ropeS = sb("ropeS", [96, 512])
                raw = [sb("raw%d" % i, [96, 512]) for i in range(2)]
                rsq = sb("rsq", [96, 512], BF16)
                hs_ = sb("hs_", [96, 512])
                qn = sb("qn", [96, 512])
                qa = sb("qa", [96, 512])
                qb_ = sb("qb_", [96, 512])
                qf = [sb("qf%d" % i, [96, 512], BF16) for i in range(2)]
                vst = [sb("vst%d" % i, [128, 512], BF16) for i in range(2)]
                ylo = [sb("ylo%d" % i, [128, 512]) for i in range(2)]
                qfr = Rot(qf, "qf")
                rawr = Rot(raw, "raw")
                vstr = Rot(vst, "vst")
                ylor = Rot(ylo, "ylo")

                def head_norm_rope(rw, rwk, gvec, gk, dst, w, p0):
                    P.act(lambda e: e.activation(out=rsq[:, 0:w], in_=rw[:, 0:w], func=AF.Square), reads=[rwk], writes=["rsq"])
                    pm, pmk = psr.next()
                    P.pe(lambda e: e.matmul(pm[0:96, 0:w], lhsT=onesb[0:96, 0:96], rhs=rsq[:, 0:w], start=True, stop=True), reads=["rsq", "onesb"], writes=[pmk])
                    P.act(lambda e: e.activation(out=hs_[:, 0:w], in_=pm[0:96, 0:w], func=AF.Sqrt, scale=1.0 / 96, bias=epsb[0:96, :]), reads=[pmk, "epsb"], writes=["hs_"])
                    P.dve(lambda e: e.reciprocal(out=hs_[:, 0:w], in_=hs_[:, 0:w]), reads=["hs_"], writes=["hs_"])
                    P.dve(lambda e: e.scalar_tensor_tensor(out=qn[:, 0:w], in0=rw[:, 0:w], scalar=gvec[:, 0, :], in1=hs_[:, 0:w], op0=ALU.mult, op1=ALU.mult),
                          reads=[rwk, "hs_", gk], writes=["qn"])
                    pr, prk = psr.next()
                    P.pe(lambda e: e.matmul(pr[0:96, 0:w], lhsT=rotf[:, :], rhs=qn[:, 0:w], start=True, stop=True), reads=["qn", "rotf"], writes=[prk])
                    P.pool(lambda e: e.tensor_tensor(out=qa[:, 0:w], in0=qn[:, 0:w], in1=ropeC[:, 0:w], op=ALU.mult), reads=["qn", "ropeC"], writes=["qa"])
                    P.dve(lambda e: e.tensor_tensor(out=qb_[:, 0:w], in0=pr[0:96, 0:w], in1=ropeS[:, 0:w], op=ALU.mult), reads=[prk, "ropeS"], writes=["qb_"])
                    qft, qfk = qfr.next()
                    P.pool(lambda e: e.tensor_tensor(out=qft[:, 0:w], in0=qa[:, 0:w], in1=qb_[:, 0:w], op=ALU.add), reads=["qa", "qb_"], writes=[qfk])
                    P.dma(lambda e: e.dma_start(out=dst[:, p0:p0 + w], in_=qft[:, 0:w]), reads=[qfk])

                for ti, (p0, w) in enumerate(tiles):
                    hi, hik = hin[0], "hin0"
                    P.dma(lambda e, hi=hi, p0=p0, w=w: e.dma_start(out=hi[:, :, 0:w], in_=hT3(hT)[:, :, p0:p0 + w]), writes=[hik])
                    P.dma(lambda e, p0=p0, w=w: e.dma_start(out=ropeC[:, 0:w], in_=c_ropeC[:, p0:p0 + w]), writes=["ropeC"], q="pool")
                    P.dma(lambda e, p0=p0, w=w: e.dma_start(out=ropeS[:, 0:w], in_=c_ropeS[:, p0:p0 + w]), writes=["ropeS"], q="pool")
                    P.act(lambda e, hi=hi, w=w: e.activation(out=hsq[:, :, 0:w], in_=hi[:, :, 0:w], func=AF.Square), reads=[hik], writes=["hsq"])
                    ps_, psk = psr.next()
                    rms_rstd(P, ps_, psk, lambda c, w=w: hsq[:, c, 0:w], "hsq", 8, w, 1.0 / D, rt, "rtA", rz, "rzA")
                    P.dve(lambda e, hi=hi, w=w: e.tensor_tensor(out=hnb[:, :, 0:w], in0=hi[:, :, 0:w], in1=rz[:, 0:w].unsqueeze(1).to_broadcast([128, 8, w]), op=ALU.mult),
                          reads=[hik, "rzA"], writes=["hnb"])
                    for oc in range(14):
                        pz, pzk = psr.next()
                        M = 96 if oc == 13 else 128
                        c0 = oc * 128
                        for c in range(8):
                            P.pe(lambda e, pz=pz, c=c, c0=c0, M=M, w=w: e.matmul(pz[0:M, 0:w], lhsT=winb[:, c, c0:c0 + M], rhs=hnb[:, c, 0:w], start=(c == 0), stop=(c == 7)),
                                 reads=["winb", "hnb"], writes=[pzk])
                        if oc < 4:
                            dst, dk = xlb[oc][:, 3:3 + w], "xlb%d" % oc
                        elif oc < 8:
                            dst, dk = zg[:, oc - 4, 0:w], "zg"
                        elif oc < 11:
                            dst, dk = zq[:, oc - 8, 0:w], "zq"
                        elif oc < 13:
                            dst, dk = zkv[:, oc - 11, 0:w], "zkv"
                        else:
                            dst, dk = zkr[:, 0:w], "zkr"
                        if oc % 2 == 0:
                            P.act(lambda e, pz=pz, dst=dst, M=M, w=w: e.activation(out=dst, in_=pz[0:M, 0:w], func=AF.Copy), reads=[pzk], writes=[dk])
                        else:
                            P.dve(lambda e, pz=pz, dst=dst, M=M, w=w: e.tensor_copy(out=dst, in_=pz[0:M, 0:w]), reads=[pzk], writes=[dk])
                        if oc < 4 and ti == 0:
                            P.pool(lambda e, oc=oc: e.memset(xlb[oc][:, 3:3 + 112], 0.0), writes=["xlb%d" % oc])
                    for g in range(4):
                        xk = "xlb%d" % g
                        xb_ = xlb[g]
                        xc, a_, a2, it, bt, hs2, gg = tA[0], tA[1], tA[2], tA[3], tA[4], tA[5], tA[6]
                        P.dve(lambda e, xb_=xb_, g=g, w=w: e.tensor_scalar(out=xc[:, 0:w], in0=xb_[:, 0:w], scalar1=v512[:, g, 0:1], scalar2=v512[:, g, 4:5], op0=ALU.mult, op1=ALU.add),
                              reads=[xk, "v512"], writes=["xc"])
                        for k in range(1, 4):
                            P.dve(lambda e, xb_=xb_, g=g, k=k, w=w: e.scalar_tensor_tensor(out=xc[:, 0:w], in0=xb_[:, k:k + w], scalar=v512[:, g, k:k + 1], in1=xc[:, 0:w], op0=ALU.mult, op1=ALU.add),
                                  reads=[xk, "v512", "xc"], writes=["xc"])
                        P.pool(lambda e, xb_=xb_, w=w: e.tensor_copy(out=xb_[:, 0:3], in_=xb_[:, w:w + 3]), reads=[xk], writes=[xk])
                        P.pool(lambda e, w=w: e.tensor_copy(out=tAb[0][:, 0:w], in_=xc[:, 0:w]), reads=["xc"], writes=["xcb"])
                        pr_, prk = psr.next()
                        pi_, pik = psr.next()
                        P.pe(lambda e, pr_=pr_, g=g, w=w: e.matmul(pr_[:, 0:w], lhsT=wab[:, g, :], rhs=tAb[0][:, 0:w], start=True, stop=True), reads=["wab", "xcb"], writes=[prk])
                        P.pe(lambda e, pi_=pi_, g=g, w=w: e.matmul(pi_[:, 0:w], lhsT=wxb[:, g, :], rhs=tAb[0][:, 0:w], start=True, stop=True), reads=["wxb", "xcb"], writes=[pik])
                        P.act(lambda e, pr_=pr_, g=g, w=w: e.activation(out=a_[:, 0:w], in_=pr_[:, 0:w], func=AF.Sigmoid, bias=v512[:, g, 5:6]), reads=[prk, "v512"], writes=["a_"])
                        P.act(lambda e, pi_=pi_, g=g, w=w: e.activation(out=it[:, 0:w], in_=pi_[:, 0:w], func=AF.Sigmoid, bias=v512[:, g, 6:7]), reads=[pik, "v512"], writes=["it"])
                        P.act(lambda e, g=g, w=w: e.activation(out=a2[:, 0:w], in_=a_[:, 0:w], func=AF.Exp, scale=nc16[:, g, :]), reads=["a_", "nc16"], writes=["a2"])
                        P.act(lambda e, g=g, w=w: e.activation(out=a_[:, 0:w], in_=a_[:, 0:w], func=AF.Exp, scale=nc8[:, g, :]), reads=["a_", "nc8", "a2"], writes=["a_"])
                        P.dve(lambda e, w=w: e.tensor_scalar(out=a2[:, 0:w], in0=a2[:, 0:w], scalar1=-1.0, scalar2=1.0, op0=ALU.mult, op1=ALU.add), reads=["a2"], writes=["a2"])
                        P.act(lambda e, w=w: e.activation(out=a2[:, 0:w], in_=a2[:, 0:w], func=AF.Sqrt), reads=["a2"], writes=["a2"])
                        P.dve(lambda e, w=w: e.tensor_tensor(out=bt[:, 0:w], in0=it[:, 0:w], in1=xc[:, 0:w], op=ALU.mult), reads=["it", "xc"], writes=["bt"])
                        P.dve(lambda e, w=w: e.tensor_tensor(out=bt[:, 0:w], in0=bt[:, 0:w], in1=a2[:, 0:w], op=ALU.mult), reads=["bt", "a2"], writes=["bt"])
                        if ti == 0:
                            P.dve(lambda e: e.memset(bt[:, 0:112], 0.0), reads=["bt"], writes=["bt"])
                        P.dve(lambda e, g=g, w=w: e.tensor_tensor_scan(out=hs2[:, 0:w], data0=a_[:, 0:w], data1=bt[:, 0:w], initial=state[:, g, :], op0=ALU.mult, op1=ALU.add),
                              reads=["a_", "bt", "state"], writes=["hs2"])
                        P.dve(lambda e, g=g, w=w: e.tensor_copy(out=state[:, g, :], in_=hs2[:, w - 1:w]), reads=["hs2"], writes=["state"])
                        P.act(lambda e, g=g, w=w: e.activation(out=gg[:, 0:w], in_=zg[:, g, 0:w], func=AF.Gelu), reads=["zg"], writes=["gg"])
                        yo, yok = ylor.next()
                        P.pool(lambda e, yo=yo, w=w: e.tensor_tensor(out=yo[:, 0:w], in0=hs2[:, 0:w], in1=gg[:, 0:w], op=ALU.mult), reads=["hs2", "gg"], writes=[yok])
                        P.dma(lambda e, yo=yo, g=g, p0=p0, w=w: e.dma_start(out=ylT[g * 128:(g + 1) * 128, p0:p0 + w], in_=yo[:, 0:w]), reads=[yok])
                    P.act(lambda e, w=w: e.activation(out=cqs[:, :, 0:w], in_=zq[:, :, 0:w], func=AF.Square), reads=["zq"], writes=["cqs"])
                    ps_, psk = psr.next()
                    rms_rstd(P, ps_, psk, lambda c, w=w: cqs[:, c, 0:w], "cqs", 3, w, 1.0 / 384, rt, "rtA", rq, "rq")
                    P.dve(lambda e, w=w: e.tensor_tensor(out=cqb[:, :, 0:w], in0=zq[:, :, 0:w], in1=rq[:, 0:w].unsqueeze(1).to_broadcast([128, 3, w]), op=ALU.mult),
                          reads=["zq", "rq"], writes=["cqb"])
                    P.act(lambda e, w=w: e.activation(out=cks[:, :, 0:w], in_=zkv[:, :, 0:w], func=AF.Square), reads=["zkv"], writes=["cks"])
                    ps_, psk = psr.next()
                    rms_rstd(P, ps_, psk, lambda c, w=w: cks[:, c, 0:w], "cks", 2, w, 1.0 / 256, rt, "rtA", rkv, "rkv")
                    P.dve(lambda e, w=w: e.tensor_tensor(out=ckb[:, :, 0:w], in0=zkv[:, :, 0:w], in1=rkv[:, 0:w].unsqueeze(1).to_broadcast([128, 2, w]), op=ALU.mult),
                          reads=["zkv", "rkv"], writes=["ckb"])
                    for h in range(8):
                        pu, puk = psr.next()
                        for c in range(3):
                            P.pe(lambda e, pu=pu, c=c, h=h, w=w: e.matmul(pu[0:96, 0:w], lhsT=wuqb[:, c, h * 96:(h + 1) * 96], rhs=cqb[:, c, 0:w], start=(c == 0), stop=(c == 2)),
                                 reads=["wuqb", "cqb"], writes=[puk])
                        rw, rwk = rawr.next()
                        P.act(lambda e, pu=pu, rw=rw, w=w: e.activation(out=rw[:, 0:w], in_=pu[0:96, 0:w], func=AF.Copy), reads=[puk], writes=[rwk])
                        head_norm_rope(rw, rwk, gqh, "gqh", qTd[h], w, p0)
                    for h in range(8):
                        pu, puk = psr.next()
                        for c in range(2):
                            P.pe(lambda e, pu=pu, c=c, h=h, w=w: e.matmul(pu[0:64, 0:w], lhsT=wukb[:, c, h, :], rhs=ckb[:, c, 0:w], start=(c == 0), stop=(c == 1)),
                                 reads=["wukb", "ckb"], writes=[puk])
                        rw, rwk = rawr.next()
                        P.act(lambda e, pu=pu, rw=rw, w=w: e.activation(out=rw[0:64, 0:w], in_=pu[0:64, 0:w], func=AF.Copy), reads=[puk], writes=[rwk])
                        P.pool(lambda e, rw=rw, w=w: e.tensor_copy(out=rw[64:96, 0:w], in_=zkr[64:96, 0:w]), reads=["zkr"], writes=[rwk])
                        head_norm_rope(rw, rwk, gkh, "gkh", kTd[h], w, p0)
                    for b in range(w // 128):
                        pv, pvk = psr.next()
                        for c in range(2):
                            P.pe(lambda e, pv=pv, c=c, b=b: e.matmul(pv[:, 0:512], lhsT=ckb[:, c, b * 128:(b + 1) * 128], rhs=wuvb[:, c, :, :].rearrange("p h d -> p (h d)"), start=(c == 0), stop=(c == 1)),
                                 reads=["wuvb", "ckb"], writes=[pvk])
                        vs, vsk = vstr.next()
                        P.act(lambda e, pv=pv, vs=vs: e.activation(out=vs[:], in_=pv[:, 0:512], func=AF.Copy), reads=[pvk], writes=[vsk])
                        P.dma(lambda e, vs=vs, b=b, p0=p0: e.dma_start(out=vvd[p0 + b * 128:p0 + (b + 1) * 128, :, :].rearrange("t h d -> t (h d)"), in_=vs[:]), reads=[vsk])
                P.emit()

            with ExitStack() as es:
                P = Prog(sy)
                sb = lambda n, s, d=F32: es.enter_context(SBT(n, s, d))
                ps_s = [es.enter_context(PST("pBs%d" % i, [128, 512], F32)) for i in range(3)]
                ps_o = [es.enter_context(PST("pBo%d" % i, [128, 512], F32)) for i in range(2)]
                ps_b = [es.enter_context(PST("pBb%d" % i, [128, 512], F32)) for i in range(2)]
                psr_s, psr_o, psr_b = Rot(ps_s, "pBs"), Rot(ps_o, "pBo"), Rot(ps_b, "pBb")
                maskf = sb("maskf", [128, 4, 512])
                maskb = sb("maskb", [128, 4, 512], BF16)
                P.dma(lambda e: e.dma_start(out=maskf[:], in_=c_mask.rearrange("j k q -> k j q")), writes=["maskf"])
                P.dve(lambda e: e.tensor_copy(out=maskb[:], in_=maskf[:]), reads=["maskf"], writes=["maskb"])
                kTh = [sb("kTh%d" % i, [96, NP], BF16) for i in range(2)]
                qTh = [sb("qTh%d" % i, [96, NP], BF16) for i in range(2)]
                v1 = [sb("v1_%d" % i, [128, NB, 65], BF16) for i in range(2)]
                pT = [sb("pT%d" % i, [128, 512], BF16) for i in range(4)]
                pTr = Rot(pT, "pT")
                num = [sb("num%d" % i, [64, 512]) for i in range(2)]
                rr_ = sb("rrB", [128, 512])
                ob = [sb("ob%d" % i, [64, 512]) for i in range(2)]
                numr, obr = Rot(num, "num"), Rot(ob, "ob")
                scale = float(1.0 / np.sqrt(96.0))
                for i in range(2):
                    P.pool(lambda e, i=i: e.memset(v1[i][:, :, 64:65], 1.0), writes=["v1_%d" % i])
                for h in range(8):
                    kt, ktk = kTh[h % 2], "kTh%d" % (h % 2)
                    qt, qtk = qTh[h % 2], "qTh%d" % (h % 2)
                    vt, vtk = v1[h % 2], "v1_%d" % (h % 2)
                    P.dma(lambda e, kt=kt, h=h: e.dma_start(out=kt[:], in_=kTd[h]), writes=[ktk])
                    P.dma(lambda e, qt=qt, h=h: e.dma_start(out=qt[:], in_=qTd[h]), writes=[qtk])
                    for b0 in range(0, NB, 13):
                        b1 = min(NB, b0 + 13)
                        P.dma(lambda e, vt=vt, h=h, b0=b0, b1=b1: e.dma_start(out=vt[:, b0:b1, 0:64], in_=vvd[b0 * 128:b1 * 128, h, :].rearrange("(b p) d -> p b d", p=128)), writes=[vtk])
                    for (p0, w) in tiles:
                        nkb = (p0 + w) // 128
                        po, pok = psr_o.next()

                        def emit_s(kb, p0=p0, w=w, kt=kt, qt=qt, ktk=ktk, qtk=qtk):
                            pss, pssk = psr_s.next()
                            diag = kb * 128 >= p0
                            P.pe(lambda e: e.matmul(pss[:, 0:w], lhsT=kt[:, kb * 128:(kb + 1) * 128], rhs=qt[:, p0:p0 + w], start=True, stop=(not diag)),
                                 reads=[ktk, qtk], writes=[pssk])
                            if diag:
                                j = kb - p0 // 128
                                P.pe(lambda e: e.matmul(pss[:, 0:w], lhsT=identb[:, :], rhs=maskb[:, j, 0:w], start=False, stop=True),
                                     reads=["identb", "maskb"], writes=[pssk])
                            return pss, pssk
                        cur = emit_s(0)
                        for kb in range(nkb):
                            nxt = emit_s(kb + 1) if kb + 1 < nkb else None
                            pss, pssk = cur
                            pt_, ptk = pTr.next()
                            if kb == 0:
                                P.act(lambda e, pss=pss, pt_=pt_, w=w: e.activation(out=pt_[:, 0:w], in_=pss[:, 0:w], func=AF.Exp, scale=scale, bias=padb[:, :]),
                                      reads=[pssk, "padb"], writes=[ptk])
                            else:
                                P.act(lambda e, pss=pss, pt_=pt_, w=w: e.activation(out=pt_[:, 0:w], in_=pss[:, 0:w], func=AF.Exp, scale=scale),
                                      reads=[pssk], writes=[ptk])
                            P.pe(lambda e, po=po, vt=vt, kb=kb, pt_=pt_, w=w, nkb=nkb: e.matmul(po[0:65, 0:w], lhsT=vt[:, kb, :], rhs=pt_[:, 0:w], start=(kb == 0), stop=(kb == nkb - 1)),
                                 reads=[vtk, ptk], writes=[pok])
                            cur = nxt
                        nm, nmk = numr.next()
                        P.act(lambda e, po=po, nm=nm, w=w: e.activation(out=nm[:, 0:w], in_=po[0:64, 0:w], func=AF.Copy), reads=[pok], writes=[nmk])
                        P.dve(lambda e, po=po, w=w: e.tensor_scalar(out=rr_[64:65, 0:w], in0=po[64:65, 0:w], scalar1=1e-30, scalar2=None, op0=ALU.max), reads=[pok], writes=["rrB"])
                        P.dve(lambda e, w=w: e.reciprocal(out=rr_[64:65, 0:w], in_=rr_[64:65, 0:w]), reads=["rrB"], writes=["rrB"])
                        pb, pbk = psr_b.next()
                        P.pe(lambda e, pb=pb, w=w: e.matmul(pb[0:64, 0:w], lhsT=onesf[64:65, 0:64], rhs=rr_[64:65, 0:w], start=True, stop=True), reads=["onesf", "rrB"], writes=[pbk])
                        o_, ok_ = obr.next()
                        P.dve(lambda e, pb=pb, nm=nm, o_=o_, w=w: e.tensor_tensor(out=o_[:, 0:w], in0=pb[0:64, 0:w], in1=nm[:, 0:w], op=ALU.mult), reads=[pbk, nmk], writes=[ok_])
                        P.dma(lambda e, o_=o_, h=h, p0=p0, w=w: e.dma_start(out=yaT[h * 64:(h + 1) * 64, p0:p0 + w], in_=o_[:, 0:w]), reads=[ok_], q="pool")
                P.emit()

            with ExitStack() as es:
                P = Prog(sy)
                sb = lambda n, s, d=F32: es.enter_context(SBT(n, s, d))
                psum = [es.enter_context(PST("pC%d" % i, [128, 512], F32)) for i in range(7)]
                psr = Rot(psum[0:7], "pC")
                pvec = es.enter_context(PST("pCv", [128, 8, 8], F32))
                glo = colvec_load(P, es, "glo", log_[l:l + 1, :], 1, 512, pvec, "pCv")
                gao = colvec_load(P, es, "gao", aog[l:l + 1, :], 1, 512, pvec, "pCv")
                gff = colvec_load(P, es, "gff", ffn_g[l:l + 1, :], 1, D, pvec, "pCv")
                wst = [sb("wstC%d" % i, [128, 2048]) for i in range(2)]
                woutb = sb("woutb", [128, 8, D], BF16)
                wqb = sb("wqb", [128, 8, 2048], BF16)
                for c in range(8):
                    ws, wk = wst[c % 2], "wstC%d" % (c % 2)
                    gsrc = glo[:, c, :] if c < 4 else gao[:, c - 4, :]
                    P.dma(lambda e, ws=ws, c=c: e.dma_start(out=ws[:, 0:D], in_=w_out[l, c * 128:(c + 1) * 128, :]), writes=[wk])
                    P.dve(lambda e, ws=ws, c=c, gsrc=gsrc: e.tensor_scalar(out=woutb[:, c, :], in0=ws[:, 0:D], scalar1=gsrc, scalar2=None, op0=ALU.mult),
                          reads=[wk, "glo", "gao"], writes=["woutb"])
                for c in range(8):
                    ws, wk = wst[c % 2], "wstC%d" % (c % 2)
                    P.dma(lambda e, ws=ws, c=c: e.dma_start(out=ws[:], in_=peer_wq[l, c * 128:(c + 1) * 128, :]), writes=[wk])
                    P.pool(lambda e, ws=ws, c=c: e.tensor_scalar(out=wqb[:, c, :], in0=ws[:], scalar1=gff[:, c, :], scalar2=None, op0=ALU.mult),
                           reads=[wk, "gff"], writes=["wqb"])
                ust = [sb("ust%d" % i, [128, D]) for i in range(2)]
                vst_ = [sb("vstC%d" % i, [128, D]) for i in range(2)]
                utb = [sb("utb%d" % i, [128, 8, 128], BF16) for i in range(2)]
                vbb = [sb("vbb%d" % i, [128, D], BF16) for i in range(2)]
                u3 = peer_u[l].rearrange("(i j) d -> j i d", j=128)
                v3 = peer_v[l].rearrange("(i j) d -> j i d", j=128)
                for j in range(128):
                    us, usk = ust[j % 2], "ust%d" % (j % 2)
                    vs, vsk = vst_[j % 2], "vstC%d" % (j % 2)
                    ub, ubk = utb[j % 2], "utb%d" % (j % 2)
                    vb, vbk = vbb[j % 2], "vbb%d" % (j % 2)
                    P.dma(lambda e, us=us, j=j: e.dma_start(out=us[:], in_=u3[j]), writes=[usk])
                    P.dma(lambda e, vs=vs, j=j: e.dma_start(out=vs[:], in_=v3[j]), writes=[vsk], q="act")
                    pa, pak = psr.next()
                    pb, pbk = psr.next()
                    for c in range(8):
                        pp_ = pa if c < 4 else pb
                        ppk = pak if c < 4 else pbk
                        P.pe(lambda e, pp_=pp_, c=c, us=us: e.transpose(out=pp_[:, (c % 4) * 128:(c % 4 + 1) * 128], in_=us[:, c * 128:(c + 1) * 128], identity=identf[:]),
                             reads=[usk, "identf"], writes=[ppk])
                    P.dve(lambda e, pa=pa, ub=ub: e.tensor_tensor(out=ub[:, 0:4, :], in0=pa[:].rearrange("p (c i) -> p c i", c=4), in1=gff[:, 0:4, :].to_broadcast([128, 4, 128]), op=ALU.mult),
                          reads=[pak, "gff"], writes=[ubk])
                    P.dve(lambda e, pb=pb, ub=ub: e.tensor_tensor(out=ub[:, 4:8, :], in0=pb[:].rearrange("p (c i) -> p c i", c=4), in1=gff[:, 4:8, :].to_broadcast([128, 4, 128]), op=ALU.mult),
                          reads=[pbk, "gff"], writes=[ubk])
                    P.pool(lambda e, vs=vs, vb=vb: e.tensor_copy(out=vb[:], in_=vs[:]), reads=[vsk], writes=[vbk])
                    P.dma(lambda e, ub=ub, j=j: e.dma_start(out=UTd[j], in_=ub[:]), reads=[ubk], q="pool")
                    P.dma(lambda e, vb=vb, j=j: e.dma_start(out=VBd[j], in_=vb[:]), reads=[vbk], q="pool")
                hin = [sb("hinC%d" % i, [128, 8, 512]) for i in range(2)]
                yl = sb("ylC", [128, 4, 512])
                ya = sb("yaC", [128, 4, 512])
                ysq = sb("ysqC", [128, 8, 512], BF16)
                ylb = sb("ylbC", [128, 4, 512], BF16)
                yab = sb("yabC", [128, 4, 512], BF16)
                rt = sb("rtC", [128, 512])
                rl = sb("rlC", [128, 512])
                ra = sb("raC", [128, 512])
                rf = sb("rfC", [128, 512])
                hnf = sb("hnfC", [128, 8, 512], BF16)
                pqo = [sb("pqo%d" % i, [128, 512], BF16) for i in range(3)]
                pqr = Rot(pqo, "pqo")
                for ti, (p0, w) in enumerate(tiles):
                    hi, hik = hin[ti % 2], "hinC%d" % (ti % 2)
                    P.dma(lambda e, hi=hi, p0=p0, w=w: e.dma_start(out=hi[:, :, 0:w], in_=hT3(hT)[:, :, p0:p0 + w]), writes=[hik])
                    P.dma(lambda e, p0=p0, w=w: e.dma_start(out=yl[:, :, 0:w], in_=hT3(ylT)[:, :, p0:p0 + w]), writes=["ylC"])
                    P.dma(lambda e, p0=p0, w=w: e.dma_start(out=ya[:, :, 0:w], in_=hT3(yaT)[:, :, p0:p0 + w]), writes=["yaC"], q="act")
                    P.act(lambda e, w=w: e.activation(out=ysq[:, 0:4, 0:w], in_=yl[:, :, 0:w], func=AF.Square), reads=["ylC"], writes=["ysqC"])
                    ps_, psk = psr.next()
                    rms_rstd(P, ps_, psk, lambda c, w=w: ysq[:, c, 0:w], "ysqC", 4, w, 1.0 / 512, rt, "rtC", rl, "rlC")
                    P.dve(lambda e, w=w: e.tensor_tensor(out=ylb[:, :, 0:w], in0=yl[:, :, 0:w], in1=rl[:, 0:w].unsqueeze(1).to_broadcast([128, 4, w]), op=ALU.mult),
                          reads=["ylC", "rlC"], writes=["ylbC"])
                    P.act(lambda e, w=w: e.activation(out=ysq[:, 4:8, 0:w], in_=ya[:, :, 0:w], func=AF.Square), reads=["yaC"], writes=["ysqC2"])
                    ps_, psk = psr.next()
                    rms_rstd(P, ps_, psk, lambda c, w=w: ysq[:, 4 + c, 0:w], "ysqC2", 4, w, 1.0 / 512, rt, "rtC", ra, "raC")
                    P.pool(lambda e, w=w: e.tensor_tensor(out=yab[:, :, 0:w], in0=ya[:, :, 0:w], in1=ra[:, 0:w].unsqueeze(1).to_broadcast([128, 4, w]), op=ALU.mult),
                           reads=["yaC", "raC"], writes=["yabC"])
                    for oc in range(8):
                        po, pok = psr.next()
                        for c in range(8):
                            rhs = ylb[:, c, 0:w] if c < 4 else yab[:, c - 4, 0:w]
                            P.pe(lambda e, po=po, c=c, oc=oc, rhs=rhs, w=w: e.matmul(po[:, 0:w], lhsT=woutb[:, c, oc * 128:(oc + 1) * 128], rhs=rhs, start=(c == 0), stop=(c == 7)),
                                 reads=["woutb", "ylbC", "yabC"], writes=[pok])
                        P.dve(lambda e, po=po, hi=hi, oc=oc, w=w: e.tensor_tensor(out=hi[:, oc, 0:w], in0=po[:, 0:w], in1=hi[:, oc, 0:w], op=ALU.add), reads=[pok, hik], writes=[hik])
                    P.dma(lambda e, hi=hi, p0=p0, w=w: e.dma_start(out=hT3(hT)[:, :, p0:p0 + w], in_=hi[:, :, 0:w]), reads=[hik], q="pool")
                    P.act(lambda e, hi=hi, w=w: e.activation(out=ysq[:, :, 0:w], in_=hi[:, :, 0:w], func=AF.Square), reads=[hik], writes=["ysqC", "ysqC2"])
                    ps_, psk = psr.next()
                    rms_rstd(P, ps_, psk, lambda c, w=w: ysq[:, c, 0:w], "ysqC", 8, w, 1.0 / D, rt, "rtC", rf, "rfC")
                    P.dve(lambda e, hi=hi, w=w: e.tensor_tensor(out=hnf[:, :, 0:w], in0=hi[:, :, 0:w], in1=rf[:, 0:w].unsqueeze(1).to_broadcast([128, 8, w]), op=ALU.mult),
                          reads=[hik, "rfC"], writes=["hnfC"])
                    P.dma(lambda e, p0=p0, w=w: e.dma_start(out=hT3(hnTd)[:, :, p0:p0 + w], in_=hnf[:, :, 0:w]), reads=["hnfC"], q="pool")
                    for gp in range(16):
                        po, pok = psr.next()
                        for c in range(8):
                            P.pe(lambda e, po=po, c=c, gp=gp, w=w: e.matmul(po[:, 0:w], lhsT=wqb[:, c, gp * 128:(gp + 1) * 128], rhs=hnf[:, c, 0:w], start=(c == 0), stop=(c == 7)),
                                 reads=["wqb", "hnfC"], writes=[pok])
                        pq_, pqk = pqr.next()
                        if gp % 2 == 0:
                            P.act(lambda e, po=po, pq_=pq_, w=w: e.activation(out=pq_[:, 0:w], in_=po[:, 0:w], func=AF.Copy), reads=[pok], writes=[pqk])
                        else:
                            P.dve(lambda e, po=po, pq_=pq_, w=w: e.tensor_copy(out=pq_[:, 0:w], in_=po[:, 0:w]), reads=[pok], writes=[pqk])
                        P.dma(lambda e, pq_=pq_, gp=gp, p0=p0, w=w: e.dma_start(out=pqTd[gp, :, p0:p0 + w], in_=pq_[:, 0:w]), reads=[pqk])
                P.emit()

            with ExitStack() as es:
                P = Prog(sy)
                sb = lambda n, s, d=F32: es.enter_context(SBT(n, s, d))
                pg = [es.enter_context(PST("pDg%d" % i, [128, 4, 128], F32)) for i in range(2)]
                ptb = [es.enter_context(PST("pDt%d" % i, [128, 8, 128], BF16)) for i in range(2)]
                pacc = [es.enter_context(PST("pDo%d" % i, [128, 512], F32)) for i in range(2)]
                pgr, ptr_ = Rot(pg, "pDg"), Rot(ptb, "pDt")
                pab = [es.enter_context(PST("pDa%d" % i, [128, 4, 128], F32)) for i in range(2)]
                par = Rot([pab[i][:, 0, :] for i in range(2)], "pDa")
                skf = sb("skf", [128, 2, 128])
                skT = sb("skT", [128, 2, 128], BF16)
                P.dma(lambda e: e.dma_start(out=skf[:], in_=peer_sk[l].rearrange("p k d -> k p d")), writes=["skf"])
                pg0, pg0k = pgr.next()
                for p_ in range(2):
                    P.pe(lambda e, p_=p_: e.transpose(out=pg0[:, p_, :], in_=skf[:, p_, :], identity=identf[:]), reads=["skf", "identf"], writes=[pg0k])
                P.dve(lambda e: e.tensor_copy(out=skT[:], in_=pg0[:, 0:2, :]), reads=[pg0k], writes=["skT"])
                hnb2 = [sb("hnbD%d" % i, [128, 8, 128], BF16) for i in range(2)]
                qTb = sb("qTD", [128, 16, 128], BF16)
                s_sb = sb("s_sb", [128, 16, 128])
                scrD = sb("scrD", [128, D])
                wk2 = [scrD[:, i * 256:(i + 1) * 256] for i in range(2)]
                top = sb("topD", [128, 16, 16])
                cand2 = [scrD[:, 512 + i * 256:512 + (i + 1) * 256] for i in range(2)]
                best = sb("bestD", [128, 8, 16])
                ebst = sb("ebstD", [128, 8, 16])
                zz = sb("zzD", [128, 8, 1])
                thr = sb("thrD", [128, 8, 16])
                e1 = sb("e1D", [128, 8, 16])
                c1 = sb("c1D", [128, 8, 1])
                e2 = sb("e2D", [128, 8, 128])
                tm = sb("tmD", [128, 64, 128], BF16)
                a1t = sb("a1tD", [128, 128, 128], BF16)
                rtt = sb("rttD", [128, 128, 128], BF16)
                gs2 = [sb("gsD%d" % i, [128, 128, 128], BF16) for i in range(2)]
                NWB = 6
                utj = [sb("utj%d" % i, [128, 8, 128], BF16) for i in range(NWB)]
                vbj = [sb("vbj%d" % i, [128, D], BF16) for i in range(NWB)]
                gat = [sb("gat%d" % i, [128, 128]) for i in range(3)]
                wT = [sb("wT%d" % i, [128, 128], BF16) for i in range(3)]
                osb = scrD
                hin = sb("hinD", [128, 8, 128])
                OSBK = ["wkD0", "wkD1", "candD0", "candD1", "hinD"]
                utr, vbr, gar, wtr = Rot(utj, "utj"), Rot(vbj, "vbj"), Rot(gat, "gat"), Rot(wT, "wT")
                wkr, cdr = Rot(wk2, "wkD"), Rot(cand2, "candD")
                s4 = s_sb[:].rearrange("t (h q) k -> t h q k", q=2)
                top4 = top[:].rearrange("t (h q) k -> t h q k", q=2)
                tmx = tm[:].rearrange("t x (h a) -> t x h a", h=8)
                evi = [0]

                def evac(out_ap, in_ap, rk, wk):
                    evi[0] += 1
                    if evi[0] % 2 == 0:
                        P.act(lambda e: e.activation(out=out_ap, in_=in_ap, func=AF.Copy), reads=rk, writes=wk)
                    else:
                        P.dve(lambda e: e.tensor_copy(out=out_ap, in_=in_ap), reads=rk, writes=wk)

                def bx(ap3, hx):
                    return ap3[:, :, hx * 64:(hx + 1) * 64].rearrange("t h x -> t x h").unsqueeze(3).to_broadcast([128, 64, 8, 16])

                def ba(ap3):
                    return ap3.unsqueeze(1).to_broadcast([128, 64, 8, 16])

                def cmp_a1(hx):
                    P.dve(lambda e: e.tensor_tensor(out=tmx, in0=bx(s4[:, :, 0, :], hx), in1=ba(top4[:, :, 0, :]), op=ALU.is_equal), reads=["s_sb", "topD"], writes=["tmD"])

                def stage1(b):
                    p0 = b * 128
                    P.dma(lambda e: e.dma_start(out=qTb[:], in_=pqTd[:, :, p0:p0 + 128].rearrange("g d t -> d g t")), writes=["qTD"], q="pool")
                    for g4 in range(4):
                        ps, psk = pgr.next()
                        for k in range(4):
                            gp = g4 * 4 + k
                            P.pe(lambda e, ps=ps, k=k, gp=gp: e.matmul(ps[:, k, :], lhsT=qTb[:, gp, :], rhs=skT[:, gp % 2, :], start=True, stop=True),
                                 reads=["qTD", "skT"], writes=[psk])
                        evac(s_sb[:, g4 * 4:(g4 + 1) * 4, :], ps[:], [psk], ["s_sb"])
                    for gp in range(16):
                        wk_, wkk = wkr.next()
                        P.dve(lambda e, gp=gp: e.max(out=top[:, gp, 0:8], in_=s_sb[:, gp, :]), reads=["s_sb"], writes=["topD"])
                        P.dve(lambda e, gp=gp, wk_=wk_: e.match_replace(out=wk_[:, 0:128], in_to_replace=top[:, gp, 0:8], in_values=s_sb[:, gp, :], imm_value=-1e30),
                              reads=["s_sb", "topD"], writes=[wkk])
                        P.dve(lambda e, gp=gp, wk_=wk_: e.max(out=top[:, gp, 8:16], in_=wk_[:, 0:128]), reads=[wkk], writes=["topD"])
                    for h in range(8):
                        cd, cdk = cdr.next()
                        wk_, wkk = wkr.next()
                        P.dve(lambda e, h=h, cd=cd: e.tensor_tensor(out=cd[:].rearrange("t (a b) -> t a b", a=16), in0=top4[:, h, 0, :].unsqueeze(2).to_broadcast([128, 16, 16]),
                                                                    in1=top4[:, h, 1, :].unsqueeze(1).to_broadcast([128, 16, 16]), op=ALU.add), reads=["topD"], writes=[cdk])
                        P.dve(lambda e, h=h, cd=cd: e.max(out=best[:, h, 0:8], in_=cd[:]), reads=[cdk], writes=["bestD"])
                        P.dve(lambda e, h=h, cd=cd, wk_=wk_: e.match_replace(out=wk_[:], in_to_replace=best[:, h, 0:8], in_values=cd[:], imm_value=-1e30),
                              reads=[cdk, "bestD"], writes=[wkk])
                        P.dve(lambda e, h=h, wk_=wk_: e.max(out=best[:, h, 8:16], in_=wk_[:]), reads=[wkk], writes=["bestD"])
                    P.dve(lambda e: e.tensor_tensor(out=ebst[:], in0=best[:], in1=best[:, :, 0:1].to_broadcast([128, 8, 16]), op=ALU.subtract), reads=["bestD"], writes=["ebstD"])
                    P.act(lambda e: e.activation(out=ebst[:], in_=ebst[:], func=AF.Exp), reads=["ebstD"], writes=["ebstD"])
                    P.dve(lambda e: e.reduce_sum(out=zz[:, :, 0], in_=ebst[:], axis=AX.X), reads=["ebstD"], writes=["zzD"])
                    P.dve(lambda e: e.reciprocal(out=zz[:], in_=zz[:]), reads=["zzD"], writes=["zzD"])
                    P.dve(lambda e: e.tensor_scalar(out=c1[:], in0=best[:, :, 15:16], scalar1=-2e-5, scalar2=None, op0=ALU.add), reads=["bestD"], writes=["c1D"])
                    P.dve(lambda e: e.tensor_tensor(out=thr[:], in0=c1[:].to_broadcast([128, 8, 16]), in1=top4[:, :, 0, :], op=ALU.subtract), reads=["c1D", "topD"], writes=["thrD"])
                    P.dve(lambda e: e.tensor_tensor(out=c1[:], in0=top4[:, :, 1, 0:1], in1=best[:, :, 0:1], op=ALU.subtract), reads=["bestD", "topD", "thrD"], writes=["c1D"])
                    P.dve(lambda e: e.tensor_tensor(out=e1[:], in0=top4[:, :, 0, :], in1=c1[:].to_broadcast([128, 8, 16]), op=ALU.add), reads=["c1D", "topD"], writes=["e1D"])
                    P.act(lambda e: e.activation(out=e1[:], in_=e1[:], func=AF.Exp), reads=["e1D"], writes=["e1D"])
                    P.dve(lambda e: e.tensor_tensor(out=e1[:], in0=e1[:], in1=zz[:].to_broadcast([128, 8, 16]), op=ALU.mult), reads=["e1D", "zzD"], writes=["e1D"])
                    P.dve(lambda e: e.tensor_tensor(out=e2[:], in0=s4[:, :, 1, :], in1=top4[:, :, 1, 0:1].to_broadcast([128, 8, 128]), op=ALU.subtract), reads=["s_sb", "topD"], writes=["e2D"])
                    P.act(lambda e: e.activation(out=e2[:], in_=e2[:], func=AF.Exp), reads=["e2D"], writes=["e2D"])
                    cmp_a1(0)

                def stage_tr(dst, dstk, hx):
                    for x8 in range(8):
                        pt_, ptk = ptr_.next()
                        for k in range(8):
                            P.pe(lambda e, pt_=pt_, k=k, x8=x8: e.transpose(out=pt_[:, k, :], in_=tm[:, x8 * 8 + k, :], identity=identb[:]), reads=["tmD", "identb"], writes=[ptk])
                        xo = hx * 64 + x8 * 8
                        evac(dst[:, :, xo:xo + 8], pt_[:].rearrange("p x t -> p t x"), [ptk], [dstk])

                def stage_r(hx):
                    P.dve(lambda e: e.tensor_tensor(out=tmx, in0=bx(s4[:, :, 1, :], hx), in1=ba(thr[:]), op=ALU.is_ge), reads=["s_sb", "thrD"], writes=["tmD"])
                    P.dve(lambda e: e.tensor_tensor(out=tmx, in0=tmx, in1=bx(e2[:], hx), op=ALU.mult), reads=["tmD", "e2D"], writes=["tmD"])
                    P.dve(lambda e: e.tensor_tensor(out=tmx, in0=tmx, in1=ba(e1[:]), op=ALU.mult), reads=["tmD", "e1D"], writes=["tmD"])

                def stage_g(b):
                    gs, gsk = gs2[b % 2], "gsD%d" % (b % 2)
                    for t4 in range(32):
                        pgt, pgk = pgr.next()
                        for k in range(4):
                            t_ = t4 * 4 + k
                            P.pe(lambda e, pgt=pgt, k=k, t_=t_: e.matmul(pgt[:, k, :], lhsT=a1t[:, t_, :], rhs=rtt[:, t_, :], start=True, stop=True), reads=["a1tD", "rttD"], writes=[pgk])
                        evac(gs[:, :, t4 * 4:(t4 + 1) * 4], pgt[:].rearrange("p t j -> p j t"), [pgk], [gsk])

                def load_block(b):
                    p0 = b * 128
                    hn, hnk = hnb2[b % 2], "hnbD%d" % (b % 2)
                    P.dma(lambda e: e.dma_start(out=hn[:], in_=hT3(hnTd)[:, :, p0:p0 + 128]), writes=[hnk], q="pool")

                pend = {}

                def main_p1(b, j):
                    hn, hnk = hnb2[b % 2], "hnbD%d" % (b % 2)
                    gs, gsk = gs2[b % 2], "gsD%d" % (b % 2)
                    ut, utk = utr.next()
                    vb, vbk = vbr.next()
                    P.dma(lambda e: e.dma_start(out=ut[:], in_=UTd[j]), writes=[utk])
                    P.dma(lambda e: e.dma_start(out=vb[:], in_=VBd[j]), writes=[vbk], q=("act" if j % 2 else "sp"))
                    pa, pak = par.next()
                    for c in range(8):
                        P.pe(lambda e, c=c: e.matmul(pa, lhsT=ut[:, c, :], rhs=hn[:, c, :], start=(c == 0), stop=(c == 7)), reads=[utk, hnk], writes=[pak])
                    ga, gak = gar.next()
                    P.act(lambda e: e.activation(out=ga[:], in_=pa, func=AF.Gelu), reads=[pak], writes=[gak])
                    wt, wtk = wtr.next()
                    P.pool(lambda e: e.tensor_tensor(out=wt[:], in0=ga[:], in1=gs[:, j, :], op=ALU.mult), reads=[gak, gsk], writes=[wtk])
                    pend[(b, j)] = (wt, wtk, vb, vbk)

                def main_p2(b, j):
                    wt, wtk, vb, vbk = pend.pop((b, j))
                    for hf in range(2):
                        P.pe(lambda e, hf=hf: e.matmul(pacc[hf][:, :], lhsT=wt[:], rhs=vb[:, hf * 512:(hf + 1) * 512], start=(j == 0), stop=(j == 127)),
                             reads=[wtk, vbk], writes=["pDo%d" % hf])

                def finish_block(b):
                    p0 = b * 128
                    P.dma(lambda e: e.dma_start(out=hin[:], in_=hT3(hT)[:, :, p0:p0 + 128]), writes=["hinD"], q="pool")
                    P.act(lambda e: e.activation(out=osb[:, 0:512], in_=pacc[0][:, :], func=AF.Copy), reads=["pDo0"], writes=OSBK)
                    P.dve(lambda e: e.tensor_copy(out=osb[:, 512:1024], in_=pacc[1][:, :]), reads=["pDo1"], writes=OSBK)
                    for hf in range(2):
                        pgt, pgk = pgr.next()
                        for k in range(4):
                            c = hf * 4 + k
                            P.pe(lambda e, pgt=pgt, k=k, c=c: e.transpose(out=pgt[:, k, :], in_=osb[:, c * 128:(c + 1) * 128], identity=identf[:]), reads=OSBK + ["identf"], writes=[pgk])
                        P.dve(lambda e, pgt=pgt, hf=hf: e.tensor_tensor(out=hin[:, hf * 4:(hf + 1) * 4, :], in0=pgt[:], in1=hin[:, hf * 4:(hf + 1) * 4, :], op=ALU.add), reads=[pgk, "hinD"], writes=["hinD"])
                    P.dma(lambda e: e.dma_start(out=hT3(hT)[:, :, p0:p0 + 128], in_=hin[:]), reads=["hinD"], q="pool")

                load_block(0)
                stage1(0)
                stage_tr(a1t, "a1tD", 0)
                cmp_a1(1)
                stage_tr(a1t, "a1tD", 1)
                stage_r(0)
                stage_tr(rtt, "rttD", 0)
                stage_r(1)
                stage_tr(rtt, "rttD", 1)
                stage_g(0)
                seq = [(b, j) for b in range(NB) for j in range(128)]
                LA = 2
                for k in range(LA):
                    main_p1(*seq[k])
                for idx, (b, j) in enumerate(seq):
                    nb_ = b + 1 if b + 1 < NB else None
                    if nb_ is not None:
                        if j == 0:
                            load_block(nb_)
                            stage1(nb_)
                        elif j == 24:
                            stage_tr(a1t, "a1tD", 0)
                            cmp_a1(1)
                        elif j == 42:
                            stage_tr(a1t, "a1tD", 1)
                            stage_r(0)
                        elif j == 62:
                            stage_tr(rtt, "rttD", 0)
                            stage_r(1)
                        elif j == 82:
                            stage_tr(rtt, "rttD", 1)
                        elif j == 100:
                            stage_g(nb_)
                    if idx + LA < len(seq):
                        main_p1(*seq[idx + LA])
                    main_p2(b, j)
                    if j == 127:
                        finish_block(b)
                P.emit()

        sfx[0] = "_F"
        with ExitStack() as es:
            P = Prog(sy)
            sb = lambda n, s, d=F32: es.enter_context(SBT(n, s, d))
            hin = [sb("hinF%d" % i, [128, 8, 128]) for i in range(2)]
            xo = [sb("xoF%d" % i, [128, D]) for i in range(2)]
            pt = [es.enter_context(PST("pFt%d" % i, [128, 4, 128], F32)) for i in range(4)]
            for b in range(1, NB):
                hi, hik = hin[b % 2], "hinF%d" % (b % 2)
                xoo, xok = xo[b % 2], "xoF%d" % (b % 2)
                P.dma(lambda e, hi=hi, b=b: e.dma_start(out=hi[:], in_=hT3(hT)[:, :, b * 128:(b + 1) * 128]), writes=[hik])
                for hf in range(2):
                    ptt, ptk = pt[(2 * b + hf) % 4], "pFt%d" % ((2 * b + hf) % 4)
                    for c in range(4):
                        P.pe(lambda e, ptt=ptt, c=c, hf=hf, hi=hi: e.transpose(out=ptt[:, c, :], in_=hi[:, hf * 4 + c, :], identity=identf[:]), reads=[hik, "identf"], writes=[ptk])
                    if hf == 0:
                        P.dve(lambda e, ptt=ptt, xoo=xoo: e.tensor_copy(out=xoo[:, 0:512], in_=ptt[:].rearrange("p c t -> p (c t)")), reads=[ptk], writes=[xok])
                    else:
                        P.act(lambda e, ptt=ptt, xoo=xoo: e.activation(out=xoo[:, 512:1024], in_=ptt[:].rearrange("p c t -> p (c t)"), func=AF.Copy), reads=[ptk], writes=[xok])
                P.dma(lambda e, xoo=xoo, b=b: e.dma_start(out=out[(b - 1) * 128:b * 128, :], in_=xoo[:]), reads=[xok], q="pool")
            P.emit()
    return nc


def host_consts(NB):
    NP = NB * 128
    ident = np.eye(128, dtype=np.float32)
    NEG = -30000.0
    mask = np.zeros((4, 128, 512), dtype=np.float32)
    k = np.arange(128)[:, None]
    for j in range(4):
        q = np.arange(512)[None, :]
        qb = q // 128
        qi = q % 128
        m = np.where(qb < j, NEG, np.where(qb == j, np.where(k > qi, NEG, 0.0), 0.0))
        mask[j] = m
    half = 16
    freq = (np.float32(10000.0) ** (-np.arange(half, dtype=np.float32) / np.float32(half))).astype(np.float32)
    pos = (np.arange(NP, dtype=np.float32) - np.float32(112.0)).astype(np.float32)
    ang = (pos[None, :] * freq[:, None]).astype(np.float32)
    C = np.ones((96, NP), dtype=np.float32)
    S = np.zeros((96, NP), dtype=np.float32)
    C[64:80] = np.cos(ang)
    C[80:96] = np.cos(ang)
    S[64:80] = np.sin(ang)
    S[80:96] = np.sin(ang)
    rot = np.zeros((96, 96), dtype=np.float32)
    for i in range(16):
        rot[80 + i, 64 + i] = -1.0
        rot[64 + i, 80 + i] = 1.0
    padb = np.zeros((128, 1), dtype=np.float32)
    padb[:112] = NEG
    return {"c_ident": ident, "c_mask": mask, "c_ropeC": C, "c_ropeS": S, "c_rot": rot, "c_padb": padb}


def make_in_maps(inputs, NB, depth, n_cores=8):
    consts = host_consts(NB)
    B = inputs["x"].shape[0]
    maps = []
    for c in range(n_cores):
        b = c % B
        m = {"x": np.ascontiguousarray(inputs["x"][b])}
        for k, v in inputs.items():
            if k == "x":
                continue
            v = np.asarray(v)
            if k in ("lru_ba", "lru_bx"):
                v = v.reshape(v.shape[0], 512)
            if k != "meta_tokens":
                v = v[:depth]
            m[k] = np.ascontiguousarray(v, dtype=np.float32)
        m.update(consts)
        maps.append(m)
    return maps


def kernel(**inputs):
    inputs = {k: np.asarray(v) for k, v in inputs.items()}
    B, SEQ, _ = inputs["x"].shape
    NB = SEQ // 128 + 1
    depth = inputs["w_in"].shape[0]
    nc = build(NB, depth)
    maps = make_in_maps(inputs, NB, depth, 8)
    res = run_bass_kernel_spmd(nc, maps, core_ids=list(range(8)))
    outs = [np.asarray(res.results[b]["out"], dtype=np.float32) for b in range(B)]
    return np.stack(outs, axis=0)
```
